# Optimizing a Trainium2 kernel written in Bass

```python
import math
import jax, jax.numpy as jnp
from jax import lax
import numpy as np

D_MODEL = 1024
BATCH = 16
SEQ = 2048
DEPTH = 2
DEC_BATCH = 8
DEC_SEQ = 16
PAST_LEN = 1024

CHUNK = 64
Q_BLOCK = 128
N_EVEN = (DEPTH + 1) // 2
N_ODD = DEPTH // 2
EPS = 1e-5
ROPE_THETA = 10000.0
N_MOD = 9
D_FF = ((8 * D_MODEL // 3 + 127) // 128) * 128
A_HEAD_DIM = 64
A_HEADS = D_MODEL // (4 * A_HEAD_DIM)
A_V_DIM = 2 * A_HEAD_DIM
A_SCALE = A_HEAD_DIM ** -0.5
MLA_HEADS = D_MODEL // 256
MLA_NOPE = 128
MLA_ROPE = 64
MLA_V = 128
MLA_Q_RANK = 3 * D_MODEL // 8
MLA_KV_RANK = D_MODEL // 4
MLA_SCALE = (MLA_NOPE + MLA_ROPE) ** -0.5
AB_IN = 2 * A_HEADS * 2 * A_HEAD_DIM + A_HEADS * A_V_DIM + MLA_Q_RANK + MLA_KV_RANK + MLA_ROPE
AB_OUT = A_HEADS * A_V_DIM + MLA_HEADS * MLA_V
C_HEAD_DIM = 64
C_HEADS = D_MODEL // C_HEAD_DIM
C_SCALE = C_HEAD_DIM ** -0.5
C_PAST_CHUNKS = 8
C_BAND_PAST = C_PAST_CHUNKS * CHUNK
C_BAND = C_BAND_PAST + CHUNK
REL_CLIP = 128

kernel_name = 'hybrid_streaming_encoder_step'


def _rmsnorm(x, g):
    xf = x.astype(jnp.float32)
    y = xf * lax.rsqrt(jnp.mean(xf * xf, axis=-1, keepdims=True) + EPS) * g.astype(jnp.float32)
    return y.astype(x.dtype)


def _modulate(x, g, shift, scale):
    return _rmsnorm(x, g) * (1 + scale[:, None, :]) + shift[:, None, :]


def _swiglu(h, w_in, w_out):
    gate, up = jnp.split(h @ w_in, 2, axis=-1)
    return (jax.nn.silu(gate) * up) @ w_out


def _rope(x, pos):
    d = x.shape[-1]
    half = d // 2
    inv = 1.0 / (ROPE_THETA ** (jnp.arange(half, dtype=jnp.float32) * (2.0 / d)))
    ang = pos.astype(jnp.float32)[:, None] * inv[None, :]
    shape = (1, pos.shape[0]) + (1,) * (x.ndim - 3) + (half,)
    cos = jnp.cos(ang).reshape(shape).astype(x.dtype)
    sin = jnp.sin(ang).reshape(shape).astype(x.dtype)
    x1, x2 = x[..., :half], x[..., half:]
    return jnp.concatenate([x1 * cos - x2 * sin, x2 * cos + x1 * sin], axis=-1)


def _sweep_query_blocks(fn, q_pos, *qs):
    L = q_pos.shape[0]
    qb = min(Q_BLOCK, L)
    nb = L // qb
    def split(a):
        return jnp.swapaxes(a.reshape((a.shape[0], nb, qb) + a.shape[2:]), 0, 1)
    outs = lax.map(fn, (q_pos.reshape(nb, qb),) + tuple(split(a) for a in qs))
    def merge(o):
        return jnp.swapaxes(o, 0, 1).reshape((o.shape[1], L) + o.shape[3:])
    return jax.tree_util.tree_map(merge, outs)


def _ab_mixer(h, pos0, past, lam_init, w_in, a_lambda, a_subln_g, q_norm_g, w_uq,
              kv_norm_g, w_ukv, w_out):
    bsz, L, _ = h.shape
    q_pos = pos0 + jnp.arange(L, dtype=jnp.int32)
    a_qk = A_HEADS * 2 * A_HEAD_DIM
    cuts = np.cumsum([a_qk, a_qk, A_HEADS * A_V_DIM, MLA_Q_RANK, MLA_KV_RANK]).tolist()
    aq, ak, av, cq, ckv, kr = jnp.split(h @ w_in, cuts, axis=-1)
    aq = _rope(aq.reshape(bsz, L, A_HEADS, 2, A_HEAD_DIM), q_pos)
    ak = _rope(ak.reshape(bsz, L, A_HEADS, 2, A_HEAD_DIM), q_pos)
    av = av.reshape(bsz, L, A_HEADS, A_V_DIM)
    bq = (_rmsnorm(cq, q_norm_g) @ w_uq).reshape(bsz, L, MLA_HEADS, MLA_NOPE + MLA_ROPE)
    bq_nope = bq[..., :MLA_NOPE]
    bq_rope = _rope(bq[..., MLA_NOPE:], q_pos)
    lat = _rmsnorm(ckv, kv_norm_g)
    kr = _rope(kr, q_pos)
    new_rows = (ak, av, lat, kr)
    if past is not None:
        ak, av, lat, kr = (jnp.concatenate([p_, n_], axis=1) for p_, n_ in zip(past, new_rows))
    Lk = ak.shape[1]
    k_pos = jnp.arange(Lk, dtype=jnp.int32)
    kv = (lat @ w_ukv).reshape(bsz, Lk, MLA_HEADS, MLA_NOPE + MLA_V)
    bk_nope, bv = kv[..., :MLA_NOPE], kv[..., MLA_NOPE:]
    lf = a_lambda.astype(jnp.float32)
    lam = jnp.exp(jnp.sum(lf[0] * lf[1])) - jnp.exp(jnp.sum(lf[2] * lf[3])) + lam_init

    def attend_block(args):
        qp, q_a, q_bn, q_br = args
        visible = (k_pos[None, :] // CHUNK) <= (qp[:, None] // CHUNK)
        s_a = jnp.einsum('bqhtd,bkhtd->bhtqk', q_a, ak).astype(jnp.float32) * A_SCALE
        p_a = jax.nn.softmax(jnp.where(visible, s_a, -jnp.inf), axis=-1)
        w_a = (p_a[:, :, 0] - lam * p_a[:, :, 1]).astype(av.dtype)
        o_a = jnp.einsum('bhqk,bkhe->bqhe', w_a, av)
        s_b = (jnp.einsum('bqhd,bkhd->bhqk', q_bn, bk_nope)
               + jnp.einsum('bqhr,bkr->bhqk', q_br, kr)).astype(jnp.float32) * MLA_SCALE
        p_b = jax.nn.softmax(jnp.where(visible, s_b, -jnp.inf), axis=-1).astype(bv.dtype)
        o_b = jnp.einsum('bhqk,bkhe->bqhe', p_b, bv)
        return (o_a, o_b)

    o_a, o_b = _sweep_query_blocks(attend_block, q_pos, aq, bq_nope, bq_rope)
    o_a = _rmsnorm(o_a, a_subln_g) * (1.0 - lam_init)
    out = jnp.concatenate([o_a.reshape(bsz, L, -1), o_b.reshape(bsz, L, -1)], axis=-1) @ w_out
    return out, new_rows


def _band_attend(q, k, v, q_pos, k_pos, rel_bias):
    qc = q_pos[:, None] // CHUNK
    kc = k_pos[None, :] // CHUNK
    valid = (kc <= qc) & (kc >= qc - C_PAST_CHUNKS) & (k_pos[None, :] >= 0)
    rel = jnp.clip(k_pos[None, :] - q_pos[:, None], -REL_CLIP, REL_CLIP) + REL_CLIP
    bias = rel_bias[:, rel].astype(jnp.float32)
    s = jnp.einsum('bqhd,bkhd->bhqk', q, k).astype(jnp.float32) * C_SCALE + bias
    p = jax.nn.softmax(jnp.where(valid, s, -jnp.inf), axis=-1).astype(v.dtype)
    return jnp.einsum('bhqk,bkhd->bqhd', p, v)


def _c_mixer(h, pos0, past, w_in, rel_bias, w_out):
    bsz, L, _ = h.shape
    q, k, v = (t.reshape(bsz, L, C_HEADS, C_HEAD_DIM) for t in jnp.split(h @ w_in, 3, axis=-1))
    q_pos = pos0 + jnp.arange(L, dtype=jnp.int32)
    if past is None:
        pad = ((0, 0), (C_BAND_PAST, 0), (0, 0), (0, 0))
        k_pad, v_pad = jnp.pad(k, pad), jnp.pad(v, pad)
        n_chunks = L // CHUNK
        q_chunks = jnp.swapaxes(q.reshape(bsz, n_chunks, CHUNK, C_HEADS, C_HEAD_DIM), 0, 1)

        def chunk_step(args):
            ci, q_c = args
            start = ci * CHUNK
            kb = lax.dynamic_slice_in_dim(k_pad, start, C_BAND, axis=1)
            vb = lax.dynamic_slice_in_dim(v_pad, start, C_BAND, axis=1)
            qp = start + jnp.arange(CHUNK, dtype=jnp.int32)
            kp = start - C_BAND_PAST + jnp.arange(C_BAND, dtype=jnp.int32)
            return _band_attend(q_c, kb, vb, qp, kp, rel_bias)

        o = lax.map(chunk_step, (jnp.arange(n_chunks, dtype=jnp.int32), q_chunks))
        o = jnp.swapaxes(o, 0, 1).reshape(bsz, L, C_HEADS * C_HEAD_DIM)
        keep = min(C_BAND_PAST, L)
        new_state = (k[:, L - keep:], v[:, L - keep:])
    else:
        pk, pv = past
        lc = pk.shape[1]
        k_all = jnp.concatenate([pk, k], axis=1)
        v_all = jnp.concatenate([pv, v], axis=1)
        k_pos = jnp.concatenate([pos0 - lc + jnp.arange(lc, dtype=jnp.int32), q_pos])
        o = _band_attend(q, k_all, v_all, q_pos, k_pos, rel_bias).reshape(bsz, L, C_HEADS * C_HEAD_DIM)
        new_state = (k_all[:, -lc:], v_all[:, -lc:])
    return o @ w_out, new_state


def _trunk(x, c, past, prm):
    pos0 = 0 if past is None else past['a_k'].shape[2]
    ab_rows, c_rows = [], []
    sc = jax.nn.silu(c)
    for l in range(DEPTH):
        mod = (sc @ prm['ada_w'][l] + prm['ada_b'][l]).reshape(c.shape[0], N_MOD, D_MODEL)
        g = prm['norm_g'][l]
        x = x + 0.5 * mod[:, 2][:, None] * _swiglu(_modulate(x, g[0], mod[:, 0], mod[:, 1]),
                                                    prm['ffn_w_in'][l, 0], prm['ffn_w_out'][l, 0])
        h = _modulate(x, g[1], mod[:, 3], mod[:, 4])
        if l % 2 == 0:
            i = l // 2
            lp = None if past is None else (past['a_k'][i], past['a_v'][i],
                                            past['mla_latent'][i], past['mla_krope'][i])
            mix, rows = _ab_mixer(h, pos0, lp, 0.8 - 0.6 * math.exp(-0.3 * l),
                                  prm['ab_w_in'][i], prm['a_lambda'][i], prm['a_subln_g'][i],
                                  prm['mla_q_norm_g'][i], prm['mla_w_uq'][i],
                                  prm['mla_kv_norm_g'][i], prm['mla_w_ukv'][i], prm['ab_w_out'][i])
            ab_rows.append(rows)
        else:
            i = l // 2
            lp = None if past is None else (past['c_k'][i], past['c_v'][i])
            mix, rows = _c_mixer(h, pos0, lp, prm['c_w_in'][i], prm['c_rel_bias'][i], prm['c_w_out'][i])
            c_rows.append(rows)
        x = x + mod[:, 5][:, None] * mix
        x = x + 0.5 * mod[:, 8][:, None] * _swiglu(_modulate(x, g[2], mod[:, 6], mod[:, 7]),
                                                    prm['ffn_w_in'][l, 1], prm['ffn_w_out'][l, 1])
    y = _rmsnorm(x, prm['final_norm_g'])
    ab_state = tuple(jnp.stack([r[j] for r in ab_rows]) for j in range(4))
    c_state = tuple(jnp.stack([r[j] for r in c_rows]) for j in range(2))
    return y, ab_state, c_state


def setup_inputs(seed: int = 0) -> dict:
    key = jax.random.key(seed)
    k = jax.random.split(key, 27)
    f32 = jnp.float32
    def nrm(kk, shape, scale):
        return jax.random.normal(kk, shape, f32) * scale
    def gain(kk, shape):
        return 1.0 + 0.02 * jax.random.normal(kk, shape, f32)
    c_len = min(C_BAND_PAST, PAST_LEN)
    return {
        'x_prompt': nrm(k[0], (BATCH, SEQ, D_MODEL), 1.0),
        'x_sample': nrm(k[1], (DEC_BATCH, DEC_SEQ, D_MODEL), 1.0),
        'cache_a_k': nrm(k[2], (N_EVEN, DEC_BATCH, PAST_LEN, A_HEADS, 2, A_HEAD_DIM), 1.0),
        'cache_a_v': nrm(k[3], (N_EVEN, DEC_BATCH, PAST_LEN, A_HEADS, A_V_DIM), 1.0),
        'cache_mla_latent': nrm(k[4], (N_EVEN, DEC_BATCH, PAST_LEN, MLA_KV_RANK), 1.0),
        'cache_mla_krope': nrm(k[5], (N_EVEN, DEC_BATCH, PAST_LEN, MLA_ROPE), 1.0),
        'cache_c_k': nrm(k[6], (N_ODD, DEC_BATCH, c_len, C_HEADS, C_HEAD_DIM), 1.0),
        'cache_c_v': nrm(k[7], (N_ODD, DEC_BATCH, c_len, C_HEADS, C_HEAD_DIM), 1.0),
        'c_prompt': nrm(k[8], (BATCH, D_MODEL), 1.0),
        'c_sample': nrm(k[9], (DEC_BATCH, D_MODEL), 1.0),
        'ada_w': nrm(k[10], (DEPTH, D_MODEL, N_MOD * D_MODEL), 0.6 * D_MODEL ** -0.5),
        'ada_b': nrm(k[11], (DEPTH, N_MOD * D_MODEL), 0.02),
        'norm_g': gain(k[12], (DEPTH, 3, D_MODEL)),
        'ffn_w_in': nrm(k[13], (DEPTH, 2, D_MODEL, 2 * D_FF), D_MODEL ** -0.5),
        'ffn_w_out': nrm(k[14], (DEPTH, 2, D_FF, D_MODEL), D_FF ** -0.5),
        'ab_w_in': nrm(k[15], (N_EVEN, D_MODEL, AB_IN), D_MODEL ** -0.5),
        'a_lambda': nrm(k[16], (N_EVEN, 4, A_HEAD_DIM), 0.1),
        'a_subln_g': gain(k[17], (N_EVEN, A_V_DIM)),
        'mla_q_norm_g': gain(k[18], (N_EVEN, MLA_Q_RANK)),
        'mla_w_uq': nrm(k[19], (N_EVEN, MLA_Q_RANK, MLA_HEADS * (MLA_NOPE + MLA_ROPE)), MLA_Q_RANK ** -0.5),
        'mla_kv_norm_g': gain(k[20], (N_EVEN, MLA_KV_RANK)),
        'mla_w_ukv': nrm(k[21], (N_EVEN, MLA_KV_RANK, MLA_HEADS * (MLA_NOPE + MLA_V)), MLA_KV_RANK ** -0.5),
        'ab_w_out': nrm(k[22], (N_EVEN, AB_OUT, D_MODEL), AB_OUT ** -0.5),
        'c_w_in': nrm(k[23], (N_ODD, D_MODEL, 3 * C_HEADS * C_HEAD_DIM), D_MODEL ** -0.5),
        'c_rel_bias': nrm(k[24], (N_ODD, C_HEADS, 2 * REL_CLIP + 1), 0.2),
        'c_w_out': nrm(k[25], (N_ODD, C_HEADS * C_HEAD_DIM, D_MODEL), (C_HEADS * C_HEAD_DIM) ** -0.5),
        'final_norm_g': gain(k[26], (D_MODEL,)),
    }


def reference(x_prompt, x_sample, cache_a_k, cache_a_v, cache_mla_latent, cache_mla_krope,
              cache_c_k, cache_c_v, c_prompt, c_sample, ada_w, ada_b, norm_g, ffn_w_in, ffn_w_out,
              ab_w_in, a_lambda, a_subln_g, mla_q_norm_g, mla_w_uq, mla_kv_norm_g, mla_w_ukv,
              ab_w_out, c_w_in, c_rel_bias, c_w_out, final_norm_g):
    prm = dict(ada_w=ada_w, ada_b=ada_b, norm_g=norm_g, ffn_w_in=ffn_w_in, ffn_w_out=ffn_w_out,
               ab_w_in=ab_w_in, a_lambda=a_lambda, a_subln_g=a_subln_g, mla_q_norm_g=mla_q_norm_g,
               mla_w_uq=mla_w_uq, mla_kv_norm_g=mla_kv_norm_g, mla_w_ukv=mla_w_ukv,
               ab_w_out=ab_w_out, c_w_in=c_w_in, c_rel_bias=c_rel_bias, c_w_out=c_w_out,
               final_norm_g=final_norm_g)
    y_prompt, ab_p, c_p = _trunk(x_prompt, c_prompt, None, prm)
    past = dict(a_k=cache_a_k, a_v=cache_a_v, mla_latent=cache_mla_latent,
                mla_krope=cache_mla_krope, c_k=cache_c_k, c_v=cache_c_v)
    y_sample, ab_s, c_s = _trunk(x_sample, c_sample, past, prm)
    a_k_p, a_v_p, lat_p, kr_p = ab_p
    a_k_s, a_v_s, lat_s, kr_s = ab_s
    c_k_p, c_v_p = c_p
    c_k_s, c_v_s = c_s
    return (y_prompt, y_sample, a_k_p, a_k_s, a_v_p, a_v_s, lat_p, lat_s, kr_p, kr_s,
            c_k_p, c_k_s, c_v_p, c_v_s)
```

```python
import math
import os
from contextlib import ExitStack
KSKIP = os.environ.get("KSKIP", "").split(",")
ATT_NB = int(os.environ.get("ATT_NB", "2"))

import numpy as np
import concourse.bass as bass
import concourse.mybir as mybir
from concourse.bass_utils import run_bass_kernel_spmd

F32 = mybir.dt.float32
BF16 = mybir.dt.bfloat16
AF = mybir.ActivationFunctionType
ALU = mybir.AluOpType
AX = mybir.AxisListType

D = 1024
NCH = 8
SEQ = 2048
DEC = 16
PAST = 1024
CPAST = 512
DFF = 2816
EPS = 1e-5
A_SCALE = 64 ** -0.5
MLA_SCALE = 192 ** -0.5
C_SCALE = 64 ** -0.5
LAM_INIT0 = 0.8 - 0.6 * math.exp(-0.3 * 0)
NDS = 40


class Buf:
    __slots__ = ("w", "r", "excl")

    def __init__(self, excl=False):
        self.w = None
        self.r = {}
        self.excl = excl


class Trk:
    def __init__(self, nc, es):
        self.nc = nc
        self.E = {}
        for n, o in (("pe", nc.tensor), ("dve", nc.vector), ("act", nc.scalar),
                     ("pool", nc.gpsimd), ("sp", nc.sync)):
            self.E[n] = dict(o=o, sem=es.enter_context(nc.semaphore("s_" + n)), cnt=0, seen={})
        self.ds = [dict(sem=es.enter_context(nc.semaphore("d%d" % i)), tot=0) for i in range(NDS)]
        self.dnext = 0
        self.dnext_pool = 0

    def wait(self, en, ev):
        if ev is None:
            return
        key, val = ev
        if key == en and en == "pe":
            return
        e = self.E[en]
        if e["seen"].get(key, 0) >= val:
            return
        sem = self.E[key]["sem"] if isinstance(key, str) else self.ds[key]["sem"]
        e["o"].wait_ge(sem, val)
        e["seen"][key] = val

    def _deps(self, en, R, W):
        for b in R:
            self.wait(en, b.w)
        for b in W:
            self.wait(en, b.w)
            for ev in list(b.r.values()):
                self.wait(en, ev)

    def _mark(self, ev, R, W):
        for b in R:
            b.r[ev[0]] = ev
        for b in W:
            b.w = ev
            b.r = {}

    @staticmethod
    def _split(R, W):
        R2 = [b for b in R if not b.excl]
        W2 = list(W) + [b for b in R if b.excl]
        return R2, W2

    def op(self, en, fn, R=(), W=()):
        R, W = self._split(R, W)
        self._deps(en, R, W)
        e = self.E[en]
        ins = fn(e["o"])
        e["cnt"] += 1
        ins.then_inc(e["sem"], 1)
        self._mark((en, e["cnt"]), R, W)
        return ins

    def dma(self, en, out, in_, R=(), W=(), **kw):
        R, W = self._split(R, W)
        half = NDS // 2
        if en == "pool":
            i = half + self.dnext_pool
            self.dnext_pool = (self.dnext_pool + 1) % half
        else:
            i = self.dnext
            self.dnext = (self.dnext + 1) % half
        d = self.ds[i]
        if d["tot"] > 0:
            self.wait(en, (i, d["tot"]))
        self._deps(en, R, W)
        ins = self.E[en]["o"].dma_start(out=out, in_=in_, **kw)
        d["tot"] += 16
        ins.then_inc(d["sem"], 16)
        self._mark((i, d["tot"]), R, W)

    def barrier(self):
        for en in self.E:
            for k2, e2 in self.E.items():
                if k2 != en and e2["cnt"] > 0:
                    self.wait(en, (k2, e2["cnt"]))
            for i, d in enumerate(self.ds):
                if d["tot"] > 0:
                    self.wait(en, (i, d["tot"]))


def _build(dbg_pass=None, dbg_nph=99):
    nc = bass.Bass("TRN2", target_bir_lowering=False)

    def din(name, shape):
        return nc.dram_tensor(name, list(shape), F32, kind="ExternalInput").ap()

    def dout(name, shape):
        return nc.dram_tensor(name, list(shape), F32, kind="ExternalOutput").ap()

    I = {}
    for name, shape in [
        ("x_p", (2, SEQ, D)), ("x_s", (DEC, D)), ("c_all", (3, D)),
        ("ca_k", (PAST, 512)), ("ca_v", (PAST, 512)), ("c_lat", (PAST, 256)), ("c_kr", (PAST, 64)),
        ("cc_k", (CPAST, D)), ("cc_v", (CPAST, D)),
        ("ada_w", (2, D, 9 * D)), ("ada_b", (2, 9 * D)), ("norm_g", (2, 3, D)),
        ("ffn_w_in", (2, 2, D, 2 * DFF)), ("ffn_w_out", (2, 2, DFF, D)),
        ("ab_w_in", (D, 2240)), ("a_lambda", (256,)), ("a_subln_g", (128,)),
        ("mla_q_norm_g", (384,)), ("mla_w_uq", (384, 768)), ("mla_kv_norm_g", (256,)),
        ("mla_w_ukv", (256, 1024)), ("ab_w_out", (D, D)), ("c_w_in", (D, 3 * D)),
        ("c_rel_bias", (16, 257)), ("c_w_out", (D, D)), ("final_norm_g", (D,)),
        ("ident", (128, 128)), ("antiid", (128, 128)),
        ("cs_p", (SEQ, 64)), ("cs_s", (DEC, 64)),
    ]:
        I[name] = din(name, shape)
    O = {}
    for name, shape in [
        ("y_p", (2, SEQ, D)), ("y_s", (DEC, D)),
        ("ak_p", (2, SEQ, 512)), ("ak_s", (DEC, 512)), ("av_p", (2, SEQ, 512)), ("av_s", (DEC, 512)),
        ("lat_p", (2, SEQ, 256)), ("lat_s", (DEC, 256)), ("kr_p", (2, SEQ, 64)), ("kr_s", (DEC, 64)),
        ("ck_p", (2, CPAST, D)), ("ck_s", (CPAST, D)), ("cv_p", (2, CPAST, D)), ("cv_s", (CPAST, D)),
    ]:
        O[name] = dout(name, shape)
    rbp = nc.dram_tensor("rbp_scratch", [16, 384], F32, kind="Internal").ap()

    with ExitStack() as es:
        tk = Trk(nc, es)

        uid = [0]

        def sb(scope, name, shape, dt=F32):
            uid[0] += 1
            return scope.enter_context(nc.sbuf_tensor("%s_%d" % (name, uid[0]), list(shape), dt))

        def mm(out, lhsT, rhs, start, stop, R, W, sgc=False):
            if sgc:
                tk.op("pe", lambda e: e.matmul(out, lhsT, rhs, start=start, stop=stop, skip_group_check=True), R, W)
            else:
                tk.op("pe", lambda e: e.matmul(out, lhsT, rhs, start=start, stop=stop), R, W)

        def tr(out, in_, idn, R, W):
            tk.op("pe", lambda e: e.transpose(out, in_, idn), R, W)

        def act(out, in_, func, R, W, bias=None, scale=1.0):
            if bias is None:
                tk.op("act", lambda e: e.activation(out, in_, func, scale=scale), R, W)
            else:
                tk.op("act", lambda e: e.activation(out, in_, func, bias=bias, scale=scale), R, W)

        def vcopy(en, out, in_, R, W):
            tk.op(en, lambda e: e.tensor_copy(out, in_), R, W)

        def tt_(en, out, a, b, op, R, W):
            tk.op(en, lambda e: e.tensor_tensor(out, a, b, op=op), R, W)

        def ts_(en, out, a, s1, s2, op0, op1, R, W):
            if op1 is None:
                tk.op(en, lambda e: e.tensor_scalar(out, a, s1, None, op0=op0), R, W)
            else:
                tk.op(en, lambda e: e.tensor_scalar(out, a, s1, s2, op0=op0, op1=op1), R, W)

        def stt(en, out, a, s, b, op0, op1, R, W):
            tk.op(en, lambda e: e.scalar_tensor_tensor(out, a, s, b, op0=op0, op1=op1), R, W)

        def recip(out, in_, R, W):
            tk.op("dve", lambda e: e.reciprocal(out, in_), R, W)

        def rpow(out, in_, R, W, power=1.0, bias=None, scale=1.0):
            if bias is None:
                tk.op("act", lambda e: e.activation(out, in_, AF.Ln, scale=scale), R, W)
            else:
                tk.op("act", lambda e: e.activation(out, in_, AF.Ln, bias=bias, scale=scale), R + [B_const], W)
            tk.op("act", lambda e: e.activation(out, out, AF.Exp, scale=-power), W, W)

        def mset(en, ap, val, W):
            tk.op(en, lambda e: e.memset(ap, val), (), W)

        top = es
        ident = sb(top, "ident", [128, 128])
        identb = sb(top, "identb", [128, 128], BF16)
        antiid = sb(top, "antiid", [128, 128])
        ones_d = sb(top, "ones_d", [128, 128], BF16)
        ones_e = sb(top, "ones_e", [128, 128], BF16)
        ones_1 = sb(top, "ones_1", [128, 128], BF16)
        epsc = sb(top, "epsc", [128, 1])
        modT = sb(top, "modT", [128, 144, 3])
        acol = sb(top, "acol", [128, 48, 3])
        gcol = sb(top, "gcol", [128, 48, 3])
        gT = sb(top, "gT", [128, 48])
        fgT = sb(top, "fgT", [128, 8])
        adabT = sb(top, "adabT", [128, 144])
        negl = sb(top, "negl", [128, 1])
        gsub = sb(top, "gsub", [128, 1])
        rb0 = sb(top, "rb0", [128, 16])
        gq_b = sb(top, "gq_b", [128, 384])
        gkv_b = sb(top, "gkv_b", [128, 256])
        cs_pt = sb(top, "cs_pt", [128, 16, 64])
        cs_st = sb(top, "cs_st", [16, 1, 64])
        xT = sb(top, "xT", [128, NCH, SEQ])
        B_const = Buf()
        B_mod = Buf()
        B_x = [Buf() for _ in range(4)]

        ps = [es.enter_context(nc.psum_tensor("ps%d" % i, [128, 512], F32)) for i in range(8)]
        psb = ps[7][:, :].bitcast(BF16)
        P = [Buf(True) for _ in range(8)]
        PB = [P[7], P[7]]

        tk.dma("sp", ident[:], I["ident"], W=[B_const])
        tk.dma("sp", antiid[:], I["antiid"], W=[B_const])
        tk.dma("pool", identb[:], I["ident"], W=[B_const])
        stA = sb(top, "stA", [128, 128])
        stB = sb(top, "stB", [128, 128])
        cT = sb(top, "cT", [128, 8, 3])
        B_stg = Buf()
        mset("dve", stB[:], 0.0, [B_stg])
        adv = I["ada_b"].rearrange("l (q p) -> (l q) p", p=128)
        tk.dma("sp", stA[:], adv[0:128, :], W=[B_stg])
        tk.dma("sp", stB[0:16, :], adv[128:144, :], W=[B_stg])
        tk.dma("sp", stB[16:64, :], I["norm_g"].rearrange("l i (c p) -> (l i c) p", p=128), W=[B_stg])
        tk.dma("sp", stB[64:72, :], I["final_norm_g"].rearrange("(c p) -> c p", p=128), W=[B_stg])
        tk.dma("sp", stB[72:96, :], I["c_all"].rearrange("s (c p) -> (s c) p", p=128), W=[B_stg])
        tk.dma("sp", stB[96:97, :], I["a_subln_g"].rearrange("(o p) -> o p", o=1), W=[B_stg])
        tr(ps[0][:, 0:128], stA[:], ident[:], [B_stg, B_const], [P[0]])
        tr(ps[1][:, 0:128], stB[:], ident[:], [B_stg, B_const], [P[1]])
        vcopy("dve", adabT[:, 0:128], ps[0][:, 0:128], [P[0]], [B_const])
        vcopy("dve", adabT[:, 128:144], ps[1][:, 0:16], [P[1]], [B_const])
        vcopy("dve", gT[:], ps[1][:, 16:64], [P[1]], [B_const])
        vcopy("dve", fgT[:], ps[1][:, 64:72], [P[1]], [B_const])
        vcopy("dve", cT[:], ps[1][:, 72:96].rearrange("p (s c) -> p c s", s=3), [P[1]], [B_mod])
        vcopy("dve", gsub[:], ps[1][:, 96:97], [P[1]], [B_const])
        tk.dma("sp", gq_b[:], I["mla_q_norm_g"].partition_broadcast(128), W=[B_const])
        tk.dma("sp", gkv_b[:], I["mla_kv_norm_g"].partition_broadcast(128), W=[B_const])
        tk.dma("sp", cs_pt[:], I["cs_p"].rearrange("(t p) f -> p t f", p=128), W=[B_const])
        tk.dma("sp", cs_st[:, 0, :], I["cs_s"], W=[B_const])
        mset("dve", ones_d[:], 1.0 / 1024.0, [B_const])
        mset("dve", ones_e[:], 1.0 / 128.0, [B_const])
        mset("dve", ones_1[:], 1.0, [B_const])
        mset("dve", epsc[:], EPS, [B_const])
        ts_("dve", gsub[:], gsub[:], 1.0 - LAM_INIT0, None, ALU.mult, None, [B_const], [B_const])

        with ExitStack() as sc:
            lamb = sb(sc, "lamb", [128, 256])
            lt = sb(sc, "lt", [128, 128])
            l2 = sb(sc, "l2", [128, 2])
            tk.dma("sp", lamb[:], I["a_lambda"].partition_broadcast(128), W=[B_const])
            tt_("dve", lt[:, 0:64], lamb[:, 0:64], lamb[:, 64:128], ALU.mult, [B_const], [B_const])
            tt_("dve", lt[:, 64:128], lamb[:, 128:192], lamb[:, 192:256], ALU.mult, [B_const], [B_const])
            tk.op("dve", lambda e: e.reduce_sum(l2[:], lt[:].rearrange("p (a b) -> p a b", a=2), axis=AX.X),
                  [B_const], [B_const])
            act(l2[:], l2[:], AF.Exp, [B_const], [B_const])
            tt_("dve", negl[:], l2[:, 1:2], l2[:, 0:1], ALU.subtract, [B_const], [B_const])
            ts_("dve", negl[:], negl[:], -LAM_INIT0, None, ALU.add, None, [B_const], [B_const])

            B_rbp, B_rbs = Buf(), Buf()
            rbs = sb(sc, "rbs", [16, 384])
            dg = sb(sc, "dg", [16, 16])
            ones_f = sb(sc, "ones_f", [16, 128])
            tk.dma("sp", rbs[:, 127:384], I["c_rel_bias"], W=[B_rbs])
            act(rbs[:, 0:127], rbs[:, 127:254], AF.Identity, [B_rbs], [B_rbs], bias=rbs[:, 127:128], scale=0.0)
            tk.dma("sp", rbp[:, :], rbs[:], R=[B_rbs], W=[B_rbp])
            mset("dve", ones_f[:], 1.0, [B_rbs])
            ts_("dve", dg[:], ident[0:16, 0:16], rbs[:, 127:128], None, ALU.mult, None, [B_rbs, B_const], [B_rbs])
            mm(ps[2][:, 0:16], ones_f[:], dg[:], True, True, [B_rbs], [P[2]])
            vcopy("dve", rb0[:], ps[2][:, 0:16], [P[2]], [B_const])
            scT = sb(sc, "scT", [128, 8, 3], BF16)
            wad = [sb(sc, "wad%d" % i, [128, 8, 1024], BF16) for i in range(2)]
            Bw = [Buf(), Buf()]
            act(scT[:], cT[:], AF.Silu, [B_mod], [B_mod])
            for l in range(2):
                for j in range(9):
                    g = l * 9 + j
                    sl = g % 2
                    tk.dma("pool", wad[sl][:],
                           I["ada_w"][l, :, j * 1024:(j + 1) * 1024].rearrange("(kc p) n -> p kc n", p=128),
                           W=[Bw[sl]])
                    pb = P[4 + g % 2]
                    pt = ps[4 + g % 2]
                    for oc in range(8):
                        for kc in range(8):
                            mm(pt[:, oc * 3:oc * 3 + 3], wad[sl][:, kc, oc * 128:(oc + 1) * 128], scT[:, kc, :],
                               kc == 0, kc == 7, [Bw[sl], B_mod], [pb])
                    tt_("dve", modT[:, g * 8:(g + 1) * 8, :],
                        pt[:, 0:24].rearrange("p (a s) -> p a s", s=3),
                        adabT[:, g * 8:(g + 1) * 8].unsqueeze(2).to_broadcast([128, 8, 3]),
                        ALU.add, [pb, B_const], [B_mod])
            for l in range(2):
                for i in range(3):
                    q = l * 3 + i
                    sc_ap = modT[:, (l * 9 + 3 * i + 1) * 8:(l * 9 + 3 * i + 2) * 8, :]
                    gt_ap = modT[:, (l * 9 + 3 * i + 2) * 8:(l * 9 + 3 * i + 3) * 8, :]
                    stt("dve", acol[:, q * 8:(q + 1) * 8, :], sc_ap, 1.0,
                        gT[:, q * 8:(q + 1) * 8].unsqueeze(2).to_broadcast([128, 8, 3]),
                        ALU.add, ALU.mult, [B_mod, B_const], [B_mod])
                    ts_("dve", gcol[:, q * 8:(q + 1) * 8, :], gt_ap, 1.0 if i == 1 else 0.5, None,
                        ALU.mult, None, [B_mod], [B_mod])
            tk.barrier()

        def shift_col(l, i, c, s):
            return modT[:, (l * 9 + 3 * i) * 8 + c, s:s + 1]

        def a_col(l, i, c, s):
            return acol[:, (l * 3 + i) * 8 + c, s:s + 1]

        def g_col(l, i, c, s):
            return gcol[:, (l * 3 + i) * 8 + c, s:s + 1]

        def seq_pass(s, T, x_src, cs_tile, caches, outs):
            tt = min(128, T)
            ntile = T // tt
            bs = min(512, T)
            nblk = T // bs
            tpb = bs // tt
            Pa = PAST if caches else 0
            Pc = CPAST if caches else 0
            is_p = caches is None

            def blk(b):
                return slice(b * bs, (b + 1) * bs)

            with ExitStack() as sc:
                xin = [sb(sc, "xin%d" % i, [128, D]) for i in range(2)]
                Bxi = [Buf(), Buf()]
                for i in range(ntile):
                    sl = i % 2
                    tk.dma("sp", xin[sl][:tt, :], x_src[i * tt:(i + 1) * tt, :], W=[Bxi[sl]])
                    for g in range(2):
                        pb, pt = P[(i * 2 + g) % 4], ps[(i * 2 + g) % 4]
                        for c4 in range(4):
                            c = g * 4 + c4
                            tr(pt[:, c4 * tt:(c4 + 1) * tt], xin[sl][:tt, c * 128:(c + 1) * 128], ident[:tt, :tt],
                               [Bxi[sl], B_const], [pb])
                        dst = xT[:, g * 4:(g + 1) * 4, i * tt:(i + 1) * tt]
                        srcv = pt[:, 0:4 * tt].rearrange("p (c t) -> p c t", c=4)
                        if g == 0:
                            vcopy("dve", dst, srcv, [pb], [B_x[(i * tt) // 512]])
                        else:
                            tk.op("act", lambda e: e.copy(dst, srcv), [pb], [B_x[(i * tt) // 512]])
                tk.barrier()

            def modulate(sc_, l, i, b, hT_ap, B_h, scr):
                sq, rstd, tmp, Bs = scr
                for c in range(8):
                    act(sq[:, c % 2, :bs], xT[:, c, blk(b)], AF.Square, [B_x[b]], [Bs[0 + c % 2]])
                    mm(ps[6][:, :bs], ones_d[:], sq[:, c % 2, :bs], c == 0, c == 7, [Bs[c % 2], B_const], [P[6]])
                rpow(rstd[:, :bs], ps[6][:, :bs], [P[6]], [Bs[2]], power=0.5, bias=epsc[:])
                for c in range(8):
                    stt("dve", tmp[:, 0, :bs], xT[:, c, blk(b)], a_col(l, i, c, s), rstd[:, :bs],
                        ALU.mult, ALU.mult, [B_x[b], B_mod, Bs[2]], [Bs[3]])
                    act(hT_ap[:, c, :bs], tmp[:, 0, :bs], AF.Identity, [Bs[3], B_mod], [B_h],
                        bias=shift_col(l, i, c, s))

            def mod_scratch(sc_):
                sq = sb(sc_, "m_sq", [128, 2, 512], BF16)
                rstd = sb(sc_, "m_rstd", [128, 512])
                tmp = sb(sc_, "m_tmp", [128, 1, 512])
                return (sq, rstd, tmp, [Buf() for _ in range(5)])

            def ffn(l, fi):
                i = 0 if fi == 0 else 2
                pieces = [(0, 4), (4, 4), (8, 4), (12, 4), (16, 3), (19, 3)]
                w_in = I["ffn_w_in"][l, fi]
                w_out = I["ffn_w_out"][l, fi]
                with ExitStack() as sc_:
                    hT = sb(sc_, "f_hT", [128, 8, T], BF16)
                    Bh = [Buf() for _ in range(nblk)]
                    actT = [sb(sc_, "f_act%d" % k, [128, 4, T], BF16) for k in range(2)]
                    Ba = [[Buf() for _ in range(nblk)] for k in range(2)]
                    wg = [sb(sc_, "f_wg%d" % k, [128, 8, 512], BF16) for k in range(2)]
                    wu = [sb(sc_, "f_wu%d" % k, [128, 8, 512], BF16) for k in range(2)]
                    wo = [sb(sc_, "f_wo%d" % k, [128, 4, D], BF16) for k in range(2)]
                    Bwg = [Buf(), Buf()]
                    Bwu = [Buf(), Buf()]
                    Bwo = [Buf(), Buf()]
                    sg = sb(sc_, "f_sg", [128, 2, 512])
                    Bsg = [Buf(), Buf()]
                    scr = mod_scratch(sc_)

                    def load(pi):
                        c0, n = pieces[pi]
                        k = pi % 2
                        tk.dma("pool", wg[k][:, :, :n * 128],
                               w_in[:, c0 * 128:(c0 + n) * 128].rearrange("(kc p) n -> p kc n", p=128), W=[Bwg[k]])
                        tk.dma("pool", wu[k][:, :, :n * 128],
                               w_in[:, DFF + c0 * 128:DFF + (c0 + n) * 128].rearrange("(kc p) n -> p kc n", p=128),
                               W=[Bwu[k]])
                        tk.dma("pool", wo[k][:, :n, :],
                               w_out[c0 * 128:(c0 + n) * 128, :].rearrange("(j p) f -> p j f", p=128), W=[Bwo[k]])

                    load(0)
                    modulate(sc_, l, i, 0, hT[:, :, blk(0)], Bh[0], scr)
                    cnt = 0
                    for pi, (c0, n) in enumerate(pieces):
                        k = pi % 2
                        if pi + 1 < len(pieces):
                            load(pi + 1)
                        for b in range(nblk):
                            if pi == 0 and b + 1 < nblk:
                                modulate(sc_, l, i, b + 1, hT[:, :, blk(b + 1)], Bh[b + 1], scr)
                            for j in range(n):
                                gb, ub = cnt % 2, 2 + cnt % 2
                                for kc in range(8):
                                    mm(ps[gb][:, :bs], wg[k][:, kc, j * 128:(j + 1) * 128], hT[:, kc, blk(b)],
                                       kc == 0, kc == 7, [Bwg[k], Bh[b]], [P[gb]])
                                for kc in range(8):
                                    mm(ps[ub][:, :bs], wu[k][:, kc, j * 128:(j + 1) * 128], hT[:, kc, blk(b)],
                                       kc == 0, kc == 7, [Bwu[k], Bh[b]], [P[ub]])
                                act(sg[:, cnt % 2, :bs], ps[gb][:, :bs], AF.Silu, [P[gb]], [Bsg[cnt % 2]])
                                tt_("dve", actT[k][:, j, blk(b)], ps[ub][:, :bs], sg[:, cnt % 2, :bs], ALU.mult,
                                    [P[ub], Bsg[cnt % 2]], [Ba[k][b]])
                                cnt += 1
                        for b in range(nblk):
                            for fo in range(8):
                                yb = (4, 5, 7)[fo % 3]
                                for j in range(n):
                                    mm(ps[yb][:, :bs], wo[k][:, j, fo * 128:(fo + 1) * 128], actT[k][:, j, blk(b)],
                                       j == 0, j == n - 1, [Bwo[k], Ba[k][b]], [P[yb]])
                                stt("dve", xT[:, fo, blk(b)], ps[yb][:, :bs], g_col(l, i, fo, s), xT[:, fo, blk(b)],
                                    ALU.mult, ALU.add, [P[yb], B_mod], [B_x[b]])
                    tk.barrier()

            def rope(en, dst, src, nmap, ti, tmpa, R, W, Bt):
                cs = cs_tile(ti)
                cosb = cs[:tt, 0:32].unsqueeze(1).to_broadcast([tt, nmap, 32])
                sinb = cs[:tt, 32:64].unsqueeze(1).to_broadcast([tt, nmap, 32])
                sv = src.rearrange("p (m two f) -> p m two f", two=2, f=32)
                dv = dst.rearrange("p (m two f) -> p m two f", two=2, f=32)
                x1, x2 = sv[:, :, 0, :], sv[:, :, 1, :]
                t = tmpa[:tt, :nmap * 64].rearrange("p (k m f) -> p k m f", k=2, f=32)
                tt_(en, t[:, 0], x1, cosb, ALU.mult, R + [B_const], [Bt])
                tt_(en, t[:, 1], x2, sinb, ALU.mult, R + [B_const], [Bt])
                tt_(en, dv[:, :, 0, :], t[:, 0], t[:, 1], ALU.subtract, [Bt], W)
                tt_(en, t[:, 0], x2, cosb, ALU.mult, R + [B_const], [Bt])
                tt_(en, t[:, 1], x1, sinb, ALU.mult, R + [B_const], [Bt])
                tt_(en, dv[:, :, 1, :], t[:, 0], t[:, 1], ALU.add, [Bt], W)

            def key_tiles(Pp):
                kts = [(j * 128, 128) for j in range(Pp // 128)]
                kts += [(Pp + j * tt, tt) for j in range(ntile)]
                return kts

            def attn_qblock(b, score_parts, v_ap_fn, kts, scale, o_bank, d_bank, Rk, pT, BpT, cntr):
                npast = len(kts) - ntile
                q0t = b * tpb
                vis = [k for k in range(len(kts)) if (k < npast or (k - npast) <= q0t + tpb - 1)]
                info = {}
                NB = ATT_NB
                SB = [0, 1, 6, 7]

                def stage_s(n_):
                    k = vis[n_]
                    koff, ks = kts[k]
                    r = 0
                    if is_p and k - npast > q0t:
                        r = k - npast - q0t
                    cs_ = slice(r * tt, bs)
                    slot = cntr[0] % (2 * NB)
                    cntr[0] += 1
                    sbk = SB[slot]
                    info[n_] = (k, ks, cs_, slot)
                    for pi_, (kT_ap, qT_ap) in enumerate(score_parts):
                        mm(ps[sbk][:ks, cs_], kT_ap[:, koff:koff + ks], qT_ap[:, b * bs + r * tt:(b + 1) * bs],
                           pi_ == 0, pi_ == len(score_parts) - 1, Rk, [P[sbk]])

                def stage_e(n_):
                    k, ks, cs_, slot = info[n_]
                    sbk = SB[slot]
                    act(pT[slot][:ks, cs_], ps[sbk][:ks, cs_], AF.Exp, [P[sbk]], [BpT[slot]], scale=scale)
                    if is_p and k - npast >= q0t:
                        r = cs_.start // tt
                        mset("pool", pT[slot][64:128, r * tt:r * tt + 64], 0.0, [BpT[slot]])

                def stage_do(n_):
                    k, ks, cs_, slot = info[n_]
                    mm(ps[d_bank][:, cs_], ones_1[:ks, :], pT[slot][:ks, cs_], n_ == 0, n_ == len(vis) - 1,
                       [BpT[slot], B_const], [P[d_bank]])
                    mm(ps[o_bank][:, cs_], v_ap_fn(k, ks), pT[slot][:ks, cs_], n_ == 0, n_ == len(vis) - 1,
                       [BpT[slot]] + Rk, [P[o_bank]])

                batches = [list(range(i0, min(i0 + NB, len(vis)))) for i0 in range(0, len(vis), NB)]

                def sb_(j):
                    for n_ in batches[j]:
                        stage_s(n_)
                    for n_ in batches[j]:
                        stage_e(n_)

                sb_(0)
                for j in range(len(batches)):
                    if j + 1 < len(batches):
                        sb_(j + 1)
                    for n_ in batches[j]:
                        stage_do(n_)

            def phase_A(oTA, B_oTA):
                l, i = 0, 1
                kts = key_tiles(Pa)
                nkt = len(kts)
                with ExitStack() as sc_:
                    aqT = sb(sc_, "a_qT", [128, 4, T], BF16)
                    akT = sb(sc_, "a_kT", [128, 4, Pa + T], BF16)
                    av = sb(sc_, "a_v", [128, nkt, 512], BF16)
                    B_q, B_k, B_v = Buf(), Buf(), Buf()
                    with ExitStack() as s2:
                        wA = sb(s2, "a_w", [128, 8, 1536], BF16)
                        B_w = Buf()
                        tk.dma("pool", wA[:, :, 0:768], I["ab_w_in"][:, 0:768].rearrange("(kc p) n -> p kc n", p=128),
                               W=[B_w])
                        tk.dma("pool", wA[:, :, 768:1536],
                               I["ab_w_in"][:, 768:1536].rearrange("(kc p) n -> p kc n", p=128), W=[B_w])
                        hTb = sb(s2, "a_hT", [128, 8, 512], BF16)
                        B_h = Buf()
                        scr = mod_scratch(s2)
                        raw = sb(s2, "a_raw", [128, 1024])
                        B_raw = Buf()
                        stg_v = [sb(s2, "a_sv%d" % k, [128, 512]) for k in range(2)]
                        stg_k = [sb(s2, "a_sk%d" % k, [128, 512]) for k in range(2)]
                        B_sv, B_sk = [Buf(), Buf()], [Buf(), Buf()]
                        qb2 = [sb(s2, "a_qb%d" % k, [128, 512], BF16) for k in range(2)]
                        kb2 = [sb(s2, "a_kb%d" % k, [128, 512], BF16) for k in range(2)]
                        B_qb2, B_kb2 = [Buf(), Buf()], [Buf(), Buf()]
                        tmq = sb(s2, "a_tmq", [128, 512])
                        tmk = sb(s2, "a_tmk", [128, 512])
                        B_tq, B_tk = Buf(), Buf()
                        if not is_p and "A_cache" not in KSKIP:
                            ckb = sb(s2, "a_ckb", [128, 8, 512], BF16)
                            B_ck = Buf()
                            tk.dma("pool", ckb[:], caches["a_k"].rearrange("(t p) f -> p t f", p=128), W=[B_ck])
                            tk.dma("pool", av[:, 0:8, :], caches["a_v"].rearrange("(t p) f -> p t f", p=128), W=[B_v])
                            for t_ in range(8):
                                pbi = t_ % 2
                                for h in range(4):
                                    tr(psb[:, pbi * 512 + h * 128: pbi * 512 + (h + 1) * 128],
                                       ckb[:, t_, h * 128:(h + 1) * 128], identb[:], [B_ck, B_const], [PB[pbi]])
                                vcopy("dve", akT[:, :, t_ * 128:(t_ + 1) * 128],
                                      psb[:, pbi * 512:(pbi + 1) * 512].rearrange("p (h t) -> p h t", h=4),
                                      [PB[pbi]], [B_k])
                        def stage1(ti):
                            b, tl = ti // tpb, ti % tpb
                            if tl == 0:
                                modulate(s2, l, i, b, hTb, B_h, scr)
                            tsl = slice(tl * tt, (tl + 1) * tt)
                            st = 3 * (ti % 2)
                            for n_ in range(3):
                                for kc in range(8):
                                    mm(ps[st + n_][:tt, :], hTb[:, kc, tsl], wA[:, kc, n_ * 512:(n_ + 1) * 512],
                                       kc == 0, kc == 7, [B_h, B_w], [P[st + n_]])
                            k2 = ti % 2
                            qb, kb, B_qb, B_kb = qb2[k2], kb2[k2], B_qb2[k2], B_kb2[k2]
                            kt_idx = Pa // 128 + ti
                            vcopy("dve", av[:tt, kt_idx, :], ps[st + 2][:tt, :], [P[st + 2]], [B_v])
                            tk.op("act", lambda e: e.copy(stg_v[k2][:tt, :], ps[st + 2][:tt, :]), [P[st + 2]], [B_sv[k2]])
                            tk.dma("sp", outs["av"][ti * tt:(ti + 1) * tt, :], stg_v[k2][:tt, :], R=[B_sv[k2]])
                            tk.op("act", lambda e: e.copy(raw[:tt, 0:512], ps[st][:tt, :]), [P[st]], [B_raw])
                            tk.op("act", lambda e: e.copy(raw[:tt, 512:1024], ps[st + 1][:tt, :]), [P[st + 1]], [B_raw])
                            rope("pool", qb[:tt, :], raw[:tt, 0:512], 8, ti, tmq, [B_raw], [B_qb], B_tq)
                            rope("dve", stg_k[k2][:tt, :], raw[:tt, 512:1024], 8, ti, tmk, [B_raw], [B_sk[k2]], B_tk)
                            tk.dma("sp", outs["ak"][ti * tt:(ti + 1) * tt, :], stg_k[k2][:tt, :], R=[B_sk[k2]])
                            vcopy("dve", kb[:tt, :], stg_k[k2][:tt, :], [B_sk[k2]], [B_kb])

                        def stage2(ti):
                            k2 = ti % 2
                            qb, kb, B_qb, B_kb = qb2[k2], kb2[k2], B_qb2[k2], B_kb2[k2]
                            for h in range(4):
                                tr(psb[:, h * 128:h * 128 + tt], qb[:tt, h * 128:(h + 1) * 128], identb[:tt, :tt],
                                   [B_qb, B_const], [PB[0]])
                            for h in range(4):
                                tr(psb[:, 512 + h * 128:512 + h * 128 + tt], kb[:tt, h * 128:(h + 1) * 128],
                                   identb[:tt, :tt], [B_kb, B_const], [PB[1]])
                            tk.op("act", lambda e: e.copy(
                                aqT[:, :, ti * tt:(ti + 1) * tt],
                                psb[:, 0:512].rearrange("p (h t) -> p h t", h=4)[:, :, :tt]), [PB[0]], [B_q])
                            vcopy("dve", akT[:, :, Pa + ti * tt:Pa + (ti + 1) * tt],
                                  psb[:, 512:1024].rearrange("p (h t) -> p h t", h=4)[:, :, :tt], [PB[1]], [B_k])

                        stage1(0)
                        for ti in range(ntile):
                            if ti + 1 < ntile:
                                stage1(ti + 1)
                            stage2(ti)
                        tk.barrier()
                    with ExitStack() as s2:
                        pT = [sb(s2, "a_pT%d" % k, [128, 512], BF16) for k in range(4)]
                        BpT = [Buf() for _ in range(4)]
                        on = [sb(s2, "a_on%d" % k, [128, 512]) for k in range(2)]
                        B_on = [Buf(), Buf()]
                        rr = sb(s2, "a_rr", [128, 512])
                        B_rr = Buf()
                        oa = sb(s2, "a_oa", [128, 512])
                        sq = sb(s2, "a_sq", [128, 512], BF16)
                        rs = sb(s2, "a_rs", [128, 512])
                        B_oa, B_sq, B_rs = Buf(), Buf(), Buf()
                        cntr = [0]
                        for b in range(nblk if "A_attn" not in KSKIP else 0):
                            for h in range(4):
                                for t_ in range(2):
                                    ob, db = 2 + t_, 4 + t_
                                    rows = slice(64 * t_, 64 * t_ + 64)
                                    attn_qblock(b, [(akT[rows, h, :], aqT[rows, h, :])],
                                                lambda k, ks: av[:ks, k, h * 128:(h + 1) * 128],
                                                kts, A_SCALE, ob, db, [B_q, B_k, B_v], pT, BpT, cntr)
                                    if "A_fin" in KSKIP:
                                        continue
                                    rpow(rr[:, :bs], ps[db][:, :bs], [P[db]], [B_rr])
                                    tt_("dve", on[t_][:, :bs], ps[ob][:, :bs], rr[:, :bs], ALU.mult,
                                        [P[ob], B_rr], [B_on[t_]])
                                if "A_fin" in KSKIP or "A_subln" in KSKIP:
                                    continue
                                stt("dve", oa[:, :bs], on[1][:, :bs], negl[:], on[0][:, :bs], ALU.mult, ALU.add,
                                    [B_on[0], B_on[1], B_const], [B_oa])
                                if "A_s1" in KSKIP:
                                    continue
                                act(sq[:, :bs], oa[:, :bs], AF.Square, [B_oa], [B_sq])
                                if "A_s2" in KSKIP:
                                    continue
                                mm(ps[5][:, :bs], ones_e[:], sq[:, :bs], True, True, [B_sq, B_const], [P[5]])
                                if "A_s3" in KSKIP:
                                    continue
                                rpow(rs[:, :bs], ps[5][:, :bs], [P[5]], [B_rs], power=0.5, bias=epsc[:])
                                if "A_s4" in KSKIP:
                                    continue
                                stt("dve", oTA[:, h, blk(b)], oa[:, :bs], gsub[:], rs[:, :bs], ALU.mult, ALU.mult,
                                    [B_oa, B_rs, B_const], [B_oTA])
                        tk.barrier()

            def phase_B(oTA, B_oTA):
                l, i = 0, 1
                kts = key_tiles(Pa)
                nkt = len(kts)
                with ExitStack() as sc_:
                    bqnT = sb(sc_, "b_qnT", [128, 4, T], BF16)
                    bqrT = sb(sc_, "b_qrT", [128, 2, T], BF16)
                    bknT = sb(sc_, "b_knT", [128, 4, Pa + T], BF16)
                    bv = sb(sc_, "b_v", [128, nkt, 512], BF16)
                    krT = sb(sc_, "b_krT", [128, Pa + T], BF16)
                    latT = sb(sc_, "b_latT", [128, 2, max(Pa, bs)], BF16)
                    B_qn, B_qr, B_kn, B_bv, B_kr, B_lat, B_wo = (Buf() for _ in range(7))
                    with ExitStack() as s2:
                        wB = sb(s2, "b_w", [128, 8, 704], BF16)
                        wuq_n = sb(s2, "b_wuqn", [128, 3, 512], BF16)
                        wuq_r = sb(s2, "b_wuqr", [128, 3, 256], BF16)
                        wkv_n = sb(s2, "b_wkvn", [128, 2, 512], BF16)
                        wkv_v = sb(s2, "b_wkvv", [128, 2, 512], BF16)
                        B_w = Buf()
                        tk.dma("pool", wB[:], I["ab_w_in"][:, 1536:2240].rearrange("(kc p) n -> p kc n", p=128), W=[B_w])
                        uq = I["mla_w_uq"].rearrange("(kc p) (h e) -> p kc h e", p=128, e=192)
                        ukv = I["mla_w_ukv"].rearrange("(kc p) (h e) -> p kc h e", p=128, e=256)
                        for kc in range(3):
                            tk.dma("pool", wuq_n[:, kc, :].rearrange("p (h e) -> p h e", e=128), uq[:, kc, :, 0:128], W=[B_w])
                            tk.dma("pool", wuq_r[:, kc, :].rearrange("p (h e) -> p h e", e=64), uq[:, kc, :, 128:192], W=[B_w])
                        for kc in range(2):
                            tk.dma("pool", wkv_n[:, kc, :].rearrange("p (h e) -> p h e", e=128), ukv[:, kc, :, 0:128], W=[B_w])
                            tk.dma("pool", wkv_v[:, kc, :].rearrange("p (h e) -> p h e", e=128), ukv[:, kc, :, 128:256], W=[B_w])
                        hTb = sb(s2, "b_hT", [128, 8, 512], BF16)
                        B_h = Buf()
                        scr = mod_scratch(s2)
                        sqs = sb(s2, "b_sqs", [128, 384])
                        ss = sb(s2, "b_ss", [128, 2])
                        B_sqs, B_ss = Buf(), Buf()
                        cqn2 = [sb(s2, "b_cqn%d" % k, [128, 384], BF16) for k in range(2)]
                        cqnT = sb(s2, "b_cqnT", [128, 3, 512], BF16)
                        B_cqn2, B_cqnT = [Buf(), Buf()], Buf()
                        stg_l = [sb(s2, "b_sl%d" % k, [128, 256]) for k in range(2)]
                        stg_r = [sb(s2, "b_sr%d" % k, [128, 64]) for k in range(2)]
                        B_sl, B_sr = [Buf(), Buf()], [Buf(), Buf()]
                        latb2 = [sb(s2, "b_latb%d" % k, [128, 256], BF16) for k in range(2)]
                        krb2 = [sb(s2, "b_krb%d" % k, [128, 128], BF16) for k in range(2)]
                        B_latb2, B_krb2 = [Buf(), Buf()], [Buf(), Buf()]
                        raw = sb(s2, "b_raw", [128, 256])
                        qrb = sb(s2, "b_qrb", [128, 256], BF16)
                        tmr = sb(s2, "b_tmr", [128, 256])
                        B_raw, B_qrb, B_tmr = Buf(), Buf(), Buf()

                        def kv_proj(col0, ncols, tiles, lat0):
                            for h in range(4):
                                pb = 4 + h % 2
                                for kc in range(2):
                                    mm(ps[pb][:, :ncols], wkv_n[:, kc, h * 128:(h + 1) * 128],
                                       latT[:, kc, lat0:lat0 + ncols], kc == 0, kc == 1, [B_w, B_lat], [P[pb]])
                                if h % 2 == 0:
                                    vcopy("dve", bknT[:, h, col0:col0 + ncols], ps[pb][:, :ncols], [P[pb]], [B_kn])
                                else:
                                    tk.op("act", lambda e: e.copy(bknT[:, h, col0:col0 + ncols], ps[pb][:, :ncols]),
                                          [P[pb]], [B_kn])
                            for (kidx, koff, ks) in tiles:
                                pb = 4 + kidx % 2
                                for kc in range(2):
                                    mm(ps[pb][:ks, :], latT[:, kc, koff:koff + ks], wkv_v[:, kc, :], kc == 0, kc == 1,
                                       [B_w, B_lat], [P[pb]])
                                vcopy("dve", bv[:ks, kidx, :], ps[pb][:ks, :], [P[pb]], [B_bv])

                        if not is_p:
                            clb = sb(s2, "b_clb", [128, 8, 256], BF16)
                            ckr = sb(s2, "b_ckr", [128, 8, 128], BF16)
                            B_cl = Buf()
                            tk.dma("pool", clb[:], caches["lat"].rearrange("(t p) f -> p t f", p=128), W=[B_cl])
                            tk.dma("pool", ckr[:, :, 0:64], caches["kr"].rearrange("(t p) f -> p t f", p=128), W=[B_cl])
                            tk.dma("pool", ckr[:, :, 64:128], caches["kr"].rearrange("(t p) f -> p t f", p=128), W=[B_cl])
                            for t_ in range(8):
                                pbi = t_ % 2
                                for c in range(2):
                                    tr(psb[:, pbi * 512 + c * 128:pbi * 512 + (c + 1) * 128], clb[:, t_, c * 128:(c + 1) * 128],
                                       identb[:], [B_cl, B_const], [PB[pbi]])
                                tr(psb[:, pbi * 512 + 256:pbi * 512 + 384], ckr[:, t_, :], identb[:], [B_cl, B_const], [PB[pbi]])
                                vcopy("dve", latT[:, :, t_ * 128:(t_ + 1) * 128],
                                      psb[:, pbi * 512:pbi * 512 + 256].rearrange("p (c t) -> p c t", c=2), [PB[pbi]], [B_lat])
                                vcopy("dve", krT[:, t_ * 128:(t_ + 1) * 128], psb[:, pbi * 512 + 256:pbi * 512 + 384],
                                      [PB[pbi]], [B_kr])
                            for hb in range(2):
                                kv_proj(hb * 512, 512, [(hb * 4 + j, hb * 512 + j * 128, 128) for j in range(4)], hb * 512)

                        def rms_tok(src_ps, n, g_b, k):
                            act(sqs[:tt, :n], src_ps, AF.Square, [P_src[0]], [B_sqs])
                            tk.op("dve", lambda e: e.reduce_sum(ss[:tt, k:k + 1], sqs[:tt, :n], axis=AX.X), [B_sqs], [B_ss])
                            rpow(ss[:tt, k:k + 1], ss[:tt, k:k + 1], [B_ss], [B_ss], power=0.5, bias=epsc[:tt, :], scale=1.0 / n)

                        P_src = [None]

                        def stage1(ti):
                            b, tl = ti // tpb, ti % tpb
                            if tl == 0:
                                modulate(s2, l, i, b, hTb, B_h, scr)
                            tsl = slice(tl * tt, (tl + 1) * tt)
                            k2 = ti % 2
                            cqn, latb, krb = cqn2[k2], latb2[k2], krb2[k2]
                            B_cqn, B_latb, B_krb = B_cqn2[k2], B_latb2[k2], B_krb2[k2]
                            b0, b1 = 2 * (ti % 2), 2 * (ti % 2) + 1
                            for kc in range(8):
                                mm(ps[b0][:tt, 0:384], hTb[:, kc, tsl], wB[:, kc, 0:384], kc == 0, kc == 7, [B_h, B_w], [P[b0]])
                            for kc in range(8):
                                mm(ps[b1][:tt, 0:320], hTb[:, kc, tsl], wB[:, kc, 384:704], kc == 0, kc == 7, [B_h, B_w], [P[b1]])
                            P_src[0] = P[b0]
                            rms_tok(ps[b0][:tt, 0:384], 384, gq_b, 0)
                            stt("dve", cqn[:tt, :], ps[b0][:tt, 0:384], ss[:tt, 0:1], gq_b[:tt, :], ALU.mult, ALU.mult,
                                [P[b0], B_ss, B_const], [B_cqn])
                            P_src[0] = P[b1]
                            rms_tok(ps[b1][:tt, 0:256], 256, gkv_b, 1)
                            stt("dve", stg_l[k2][:tt, :], ps[b1][:tt, 0:256], ss[:tt, 1:2], gkv_b[:tt, :], ALU.mult, ALU.mult,
                                [P[b1], B_ss, B_const], [B_sl[k2]])
                            tk.dma("sp", outs["lat"][ti * tt:(ti + 1) * tt, :], stg_l[k2][:tt, :], R=[B_sl[k2]])
                            tk.op("act", lambda e: e.copy(latb[:tt, :], stg_l[k2][:tt, :]), [B_sl[k2]], [B_latb])
                            tk.op("act", lambda e: e.copy(raw[:tt, 0:64], ps[b1][:tt, 256:320]), [P[b1]], [B_raw])
                            rope("pool", stg_r[k2][:tt, :], raw[:tt, 0:64], 1, ti, tmr, [B_raw], [B_sr[k2]], B_tmr)
                            tk.dma("sp", outs["kr"][ti * tt:(ti + 1) * tt, :], stg_r[k2][:tt, :], R=[B_sr[k2]])
                            tk.op("pool", lambda e: e.tensor_copy(krb[:tt, 0:64], stg_r[k2][:tt, :]), [B_sr[k2]], [B_krb])
                            tk.op("pool", lambda e: e.tensor_copy(krb[:tt, 64:128], stg_r[k2][:tt, :]), [B_sr[k2]], [B_krb])

                        def stage2(ti):
                            b, tl = ti // tpb, ti % tpb
                            tsl = slice(tl * tt, (tl + 1) * tt)
                            k2 = ti % 2
                            cqn, latb, krb = cqn2[k2], latb2[k2], krb2[k2]
                            B_cqn, B_latb, B_krb = B_cqn2[k2], B_latb2[k2], B_krb2[k2]
                            for c in range(3):
                                tr(psb[:, c * 128:c * 128 + tt], cqn[:tt, c * 128:(c + 1) * 128], identb[:tt, :tt],
                                   [B_cqn, B_const], [PB[0]])
                            for c in range(2):
                                tr(psb[:, 512 + c * 128:512 + c * 128 + tt], latb[:tt, c * 128:(c + 1) * 128], identb[:tt, :tt],
                                   [B_latb, B_const], [PB[1]])
                            tr(psb[:, 768:768 + tt], krb[:tt, :], identb[:tt, :tt], [B_krb, B_const], [PB[1]])
                            vcopy("dve", cqnT[:, :, tsl], psb[:, 0:384].rearrange("p (c t) -> p c t", c=3)[:, :, :tt],
                                  [PB[0]], [B_cqnT])
                            kc0 = Pa + ti * tt
                            tk.op("act", lambda e: e.copy(
                                latT[:, :, tl * tt:(tl + 1) * tt], psb[:, 512:768].rearrange("p (c t) -> p c t", c=2)[:, :, :tt]),
                                [PB[1]], [B_lat])
                            tk.op("act", lambda e: e.copy(krT[:, kc0:kc0 + tt], psb[:, 768:768 + tt]), [PB[1]], [B_kr])
                            if tl == tpb - 1:
                                for h in range(4):
                                    pb = 4 + h % 2
                                    for kc in range(3):
                                        mm(ps[pb][:, :bs], wuq_n[:, kc, h * 128:(h + 1) * 128], cqnT[:, kc, :bs],
                                           kc == 0, kc == 2, [B_w, B_cqnT], [P[pb]])
                                    vcopy("dve", bqnT[:, h, blk(b)], ps[pb][:, :bs], [P[pb]], [B_qn])
                                for tl2 in range(tpb):
                                    ti2 = b * tpb + tl2
                                    pb = 4 + tl2 % 2
                                    for kc in range(3):
                                        mm(ps[pb][:tt, 0:256], cqnT[:, kc, tl2 * tt:(tl2 + 1) * tt], wuq_r[:, kc, :],
                                           kc == 0, kc == 2, [B_w, B_cqnT], [P[pb]])
                                    tk.op("act", lambda e: e.copy(raw[:tt, :], ps[pb][:tt, 0:256]), [P[pb]], [B_raw])
                                    rope("pool", qrb[:tt, :], raw[:tt, :], 4, ti2, tmr, [B_raw], [B_qrb], B_tmr)
                                    for c in range(2):
                                        tr(psb[:, c * 128:c * 128 + tt], qrb[:tt, c * 128:(c + 1) * 128], identb[:tt, :tt],
                                           [B_qrb, B_const], [PB[0]])
                                    vcopy("dve", bqrT[:, :, ti2 * tt:(ti2 + 1) * tt],
                                          psb[:, 0:256].rearrange("p (c t) -> p c t", c=2)[:, :, :tt], [PB[0]], [B_qr])
                                kv_proj(Pa + b * bs, bs,
                                        [(Pa // 128 + b * tpb + j, j * tt, tt) for j in range(tpb)], 0)

                        stage1(0)
                        for ti in range(ntile):
                            if ti + 1 < ntile:
                                stage1(ti + 1)
                            stage2(ti)
                        tk.barrier()
                    if "B_attn" in KSKIP:
                        return
                    with ExitStack() as s2:
                        wout = sb(s2, "b_wout", [128, 8, D], BF16)
                        tk.dma("pool", wout[:], I["ab_w_out"].rearrange("(kc p) n -> p kc n", p=128), W=[B_wo])
                        pT = [sb(s2, "b_pT%d" % k, [128, 512], BF16) for k in range(4)]
                        BpT = [Buf() for _ in range(4)]
                        rr = sb(s2, "b_rr", [128, 512])
                        B_rr = Buf()
                        oTB = sb(s2, "b_oTB", [128, 4, 512], BF16)
                        B_oTB = Buf()
                        cntr = [0]
                        for b in range(nblk):
                            for h in range(4):
                                ob, db = 2 + h % 2, 4 + h % 2
                                rows = slice(64 * (h % 2), 64 * (h % 2) + 64)
                                attn_qblock(b, [(bknT[:, h, :], bqnT[:, h, :]), (krT[rows, :], bqrT[rows, h // 2, :])],
                                            lambda k, ks: bv[:ks, k, h * 128:(h + 1) * 128],
                                            kts, MLA_SCALE, ob, db, [B_qn, B_qr, B_kn, B_kr, B_bv], pT, BpT, cntr)
                                rpow(rr[:, :bs], ps[db][:, :bs], [P[db]], [B_rr])
                                tt_("dve", oTB[:, h, :bs], ps[ob][:, :bs], rr[:, :bs], ALU.mult, [P[ob], B_rr], [B_oTB])
                            for fo in range(8):
                                wb_ = 2 + fo % 4
                                for c in range(8):
                                    rhs = oTA[:, c, blk(b)] if c < 4 else oTB[:, c - 4, :bs]
                                    mm(ps[wb_][:, :bs], wout[:, c, fo * 128:(fo + 1) * 128], rhs, c == 0, c == 7,
                                       [B_wo, B_oTA, B_oTB], [P[wb_]])
                                stt("dve", xT[:, fo, blk(b)], ps[wb_][:, :bs], g_col(l, i, fo, s), xT[:, fo, blk(b)],
                                    ALU.mult, ALU.add, [P[wb_], B_mod], [B_x[b]])
                        tk.barrier()

            def phase_C():
                l, i = 1, 1
                kts = key_tiles(Pc)
                nkt = len(kts)
                npast = Pc // 128
                with ExitStack() as sc_:
                    hT = sb(sc_, "c_hT", [128, 8, T], BF16)
                    Bh = [Buf() for _ in range(nblk)]
                    biasT = sb(sc_, "biasT", [128, 16, 2, 128], BF16)
                    B_bias = Buf()
                    with ExitStack() as s0:
                        scr = mod_scratch(s0)
                        for b in range(nblk):
                            modulate(s0, l, i, b, hT[:, :, blk(b)], Bh[b], scr)
                        hk = sb(s0, "hk", [128, 16, 128])
                        B_rb0, B_hk = B_const, Buf()
                        for dl in range(2 if "C_bias" not in KSKIP else 0):
                            off = 128 if dl == 0 else 0
                            src = bass.AP(tensor=rbp.tensor, offset=off, ap=[[1, 128], [384, 16], [1, 128]])
                            tk.dma("sp", hk[:], src, W=[B_hk])
                            for h in range(16):
                                mm(ps[h % 4][:, 0:128], hk[:, h, :], antiid[:], True, True, [B_hk, B_const], [P[h % 4]])
                                ts_("dve", biasT[:, h, dl, :], ps[h % 4][:, 0:128], rb0[:, h:h + 1], 1.0 / C_SCALE,
                                    ALU.subtract, ALU.mult, [P[h % 4], B_rb0], [B_bias])
                        tk.barrier()
                    for hh in range(2):
                        with ExitStack() as s1:
                            qT = sb(s1, "c_qT", [128, 4, T], BF16)
                            kT = sb(s1, "c_kT", [128, 4, Pc + T], BF16)
                            v = sb(s1, "c_v", [128, nkt, 512], BF16)
                            B_q, B_k, B_v = Buf(), Buf(), Buf()
                            with ExitStack() as s2:
                                wq = sb(s2, "c_wq", [128, 8, 512], BF16)
                                wk = sb(s2, "c_wk", [128, 8, 512], BF16)
                                wv = sb(s2, "c_wv", [128, 8, 512], BF16)
                                B_w = Buf()
                                for wt, c0 in ((wq, 0), (wk, D), (wv, 2 * D)):
                                    tk.dma("pool", wt[:], I["c_w_in"][:, c0 + hh * 512:c0 + (hh + 1) * 512].rearrange(
                                        "(kc p) n -> p kc n", p=128), W=[B_w])
                                stg = [sb(s2, "c_st%d" % k, [128, 512]) for k in range(2)]
                                B_st = [Buf() for _ in range(2)]
                                if not is_p:
                                    ckb = sb(s2, "c_ckb", [128, 4, 512], BF16)
                                    B_ck = Buf()
                                    tk.dma("pool", ckb[:], caches["c_k"][:, hh * 512:(hh + 1) * 512].rearrange(
                                        "(t p) f -> p t f", p=128), W=[B_ck])
                                    tk.dma("pool", v[:, 0:4, :], caches["c_v"][:, hh * 512:(hh + 1) * 512].rearrange(
                                        "(t p) f -> p t f", p=128), W=[B_v])
                                    for t_ in range(4):
                                        pbi = t_ % 2
                                        for c in range(4):
                                            tr(psb[:, pbi * 512 + c * 128:pbi * 512 + (c + 1) * 128],
                                               ckb[:, t_, c * 128:(c + 1) * 128], identb[:], [B_ck, B_const], [PB[pbi]])
                                        vcopy("dve", kT[:, :, t_ * 128:(t_ + 1) * 128],
                                              psb[:, pbi * 512:(pbi + 1) * 512].rearrange("p (c t) -> p c t", c=4),
                                              [PB[pbi]], [B_k])
                                    if hh == 0 and "C_d2d" not in KSKIP:
                                        tk.dma("sp", outs["ck"][0:CPAST - DEC, :], caches["c_k"][DEC:CPAST, :])
                                        tk.dma("sp", outs["cv"][0:CPAST - DEC, :], caches["c_v"][DEC:CPAST, :])
                                sti = 0
                                for b in range(nblk):
                                    for c in range(4):
                                        for (wt, dst, Bd, off, pb) in ((wq, qT, B_q, 0, 0), (wk, kT, B_k, Pc, 1)):
                                            pbk = pb * 2 + c % 2
                                            for kc in range(8):
                                                mm(ps[pbk][:, :bs], wt[:, kc, c * 128:(c + 1) * 128], hT[:, kc, blk(b)],
                                                   kc == 0, kc == 7, [B_w, Bh[b]], [P[pbk]])
                                            dsl = dst[:, c, off + b * bs:off + (b + 1) * bs]
                                            if pb == 0:
                                                vcopy("dve", dsl, ps[pbk][:, :bs], [P[pbk]], [Bd])
                                            else:
                                                tk.op("act", lambda e: e.copy(dsl, ps[pbk][:, :bs]), [P[pbk]], [Bd])
                                    for tl in range(tpb):
                                        ti = b * tpb + tl
                                        tsl = slice(ti * tt, (ti + 1) * tt)
                                        pbk = 4 + ti % 2
                                        for kc in range(8):
                                            mm(ps[pbk][:tt, :], hT[:, kc, tsl], wv[:, kc, :], kc == 0, kc == 7, [Bh[b], B_w], [P[pbk]])
                                        vcopy("dve", v[:tt, npast + ti, :], ps[pbk][:tt, :], [P[pbk]], [B_v])
                                        orow = (ti * tt - (T - CPAST)) if is_p else (CPAST - DEC)
                                        if orow >= 0:
                                            k4 = sti % 2
                                            sti += 1
                                            tk.op("act", lambda e: e.copy(stg[k4][:tt, :], ps[pbk][:tt, :]), [P[pbk]], [B_st[k4]])
                                            tk.dma("sp", outs["cv"][orow:orow + tt, hh * 512:(hh + 1) * 512], stg[k4][:tt, :],
                                                   R=[B_st[k4]])
                                            for kc in range(8):
                                                mm(ps[6][:tt, :], hT[:, kc, tsl], wk[:, kc, :], kc == 0, kc == 7, [Bh[b], B_w], [P[6]])
                                            k4 = sti % 2
                                            sti += 1
                                            tk.op("act", lambda e: e.copy(stg[k4][:tt, :], ps[6][:tt, :]), [P[6]], [B_st[k4]])
                                            tk.dma("sp", outs["ck"][orow:orow + tt, hh * 512:(hh + 1) * 512], stg[k4][:tt, :],
                                                   R=[B_st[k4]])
                                tk.barrier()
                            with ExitStack() as s2:
                                oTC = sb(s2, "c_oT", [128, 4, T], BF16)
                                B_oTC = Buf()
                                wout = sb(s2, "c_wout", [128, 4, D], BF16)
                                B_wo = Buf()
                                tk.dma("pool", wout[:], I["c_w_out"][hh * 512:(hh + 1) * 512, :].rearrange(
                                    "(kc p) n -> p kc n", p=128), W=[B_wo])
                                pT = [sb(s2, "c_pT%d" % k, [128, 4, 128], BF16) for k in range(2)]
                                BpT = [Buf(), Buf()]
                                rr = sb(s2, "c_rr", [128, 4, 128])
                                B_rr = Buf()
                                units = []
                                for ti in range(ntile):
                                    if is_p:
                                        vis = [(k, k - ti) for k in range(max(0, ti - 4), ti + 1)]
                                    else:
                                        vis = [(k, None) for k in range(nkt)]
                                    for g2 in range(2):
                                        for n_, (k, dl) in enumerate(vis):
                                            units.append((ti, g2, n_, len(vis), k, dl))

                                def stage_s(u):
                                    ti, g2, n_, nv, k, dl = units[u]
                                    koff, ks = kts[k]
                                    sbk = u % 2
                                    if is_p:
                                        bidx = 0 if dl == 0 else (1 if dl == -1 else None)
                                    else:
                                        bidx = 1 if k == npast - 1 else (0 if k == npast else None)
                                    for hl in range(4):
                                        hd = 4 * g2 + hl
                                        par, slot = hl % 2, hl // 2
                                        sb_ = 2 * sbk + par
                                        c, rows = hd // 2, slice(64 * par, 64 * par + 64)
                                        so = ps[sb_][:ks, slot * 128:slot * 128 + tt]
                                        mm(so, kT[rows, c, koff:koff + ks], qT[rows, c, ti * tt:(ti + 1) * tt],
                                           True, bidx is None, [B_q, B_k], [P[sb_]])
                                        if bidx is not None:
                                            mm(so, identb[:ks, :ks], biasT[:ks, hh * 8 + hd, bidx, :tt],
                                               False, True, [B_const, B_bias], [P[sb_]])
                                    for par in range(2):
                                        sb_ = 2 * sbk + par
                                        act(pT[sbk][:ks, 2 * par:2 * par + 2, :tt],
                                            ps[sb_][:ks, 0:256].rearrange("p (h t) -> p h t", h=2)[:, :, :tt],
                                            AF.Exp, [P[sb_]], [BpT[sbk]], scale=C_SCALE)
                                    if is_p and dl == 0:
                                        mset("pool", pT[sbk][64:128, :, 0:64], 0.0, [BpT[sbk]])
                                    if is_p and dl == -4:
                                        mset("pool", pT[sbk][0:64, :, 64:128], 0.0, [BpT[sbk]])

                                def stage_do(u):
                                    ti, g2, n_, nv, k, dl = units[u]
                                    koff, ks = kts[k]
                                    sbk = u % 2
                                    ob, db = 4 + g2, 6 + g2
                                    dv_ = ps[db][:, :].rearrange("p (h t) -> p h t", h=4)
                                    ov_ = ps[ob][:, :].rearrange("p (h t) -> p h t", h=4)
                                    if tt == 128:
                                        mm(ps[db][:, :], ones_1[:ks, :], pT[sbk][:ks, :, :].rearrange("p h t -> p (h t)"),
                                           n_ == 0, n_ == nv - 1, [BpT[sbk], B_const], [P[db]])
                                    else:
                                        for j in range(4):
                                            mm(dv_[:, j, :tt], ones_1[:ks, :], pT[sbk][:ks, j, :tt],
                                               n_ == 0 and j == 0, n_ == nv - 1, [BpT[sbk], B_const], [P[db]], sgc=True)
                                    for hl in range(4):
                                        hd = 4 * g2 + hl
                                        j = (hl % 2) * 2 + hl // 2
                                        c = hd // 2
                                        mm(ov_[:, j, :tt], v[:ks, k, c * 128:(c + 1) * 128], pT[sbk][:ks, j, :tt],
                                           n_ == 0 and hl == 0, n_ == nv - 1, [BpT[sbk], B_v], [P[ob]], sgc=True)
                                    if n_ == nv - 1:
                                        rpow(rr[:, :, :tt], dv_[:, :, :tt], [P[db]], [B_rr])
                                        for hl in range(4):
                                            hd = 4 * g2 + hl
                                            j = (hl % 2) * 2 + hl // 2
                                            c, rows = hd // 2, slice(64 * (hd % 2), 64 * (hd % 2) + 64)
                                            tt_("dve", oTC[rows, c, ti * tt:(ti + 1) * tt], ov_[rows, j, :tt],
                                                rr[rows, j, :tt], ALU.mult, [P[ob], B_rr], [B_oTC])

                                stage_s(0)
                                for u in range(len(units)):
                                    if u + 1 < len(units):
                                        stage_s(u + 1)
                                    stage_do(u)
                                for b in range(nblk if "C_noout" not in KSKIP else 0):
                                    for fo in range(8):
                                        pb = fo % 2
                                        for c in range(4):
                                            mm(ps[pb][:, :bs], wout[:, c, fo * 128:(fo + 1) * 128], oTC[:, c, blk(b)], c == 0, c == 3,
                                               [B_wo, B_oTC], [P[pb]])
                                        stt("dve", xT[:, fo, blk(b)], ps[pb][:, :bs], g_col(l, i, fo, s), xT[:, fo, blk(b)],
                                            ALU.mult, ALU.add, [P[pb], B_mod], [B_x[b]])
                                tk.barrier()

            def final():
                with ExitStack() as sc_:
                    sq = sb(sc_, "y_sq", [128, 2, 512], BF16)
                    rstd = sb(sc_, "y_rstd", [128, 512])
                    yT = sb(sc_, "y_yT", [128, 8, 512])
                    stg = [sb(sc_, "y_st%d" % k, [128, D]) for k in range(2)]
                    Bs = [Buf() for _ in range(3)]
                    B_y = Buf()
                    B_st = [Buf(), Buf()]
                    for b in range(nblk):
                        for c in range(8):
                            act(sq[:, c % 2, :bs], xT[:, c, blk(b)], AF.Square, [B_x[b]], [Bs[c % 2]])
                            mm(ps[6][:, :bs], ones_d[:], sq[:, c % 2, :bs], c == 0, c == 7, [Bs[c % 2], B_const], [P[6]])
                        rpow(rstd[:, :bs], ps[6][:, :bs], [P[6]], [Bs[2]], power=0.5, bias=epsc[:])
                        for c in range(8):
                            stt("dve", yT[:, c, :bs], xT[:, c, blk(b)], fgT[:, c:c + 1], rstd[:, :bs], ALU.mult, ALU.mult,
                                [B_x[b], B_const, Bs[2]], [B_y])
                        for tl in range(tpb):
                            ti = b * tpb + tl
                            k2 = ti % 2
                            for g in range(2):
                                pb = g * 2 + k2
                                for c4 in range(4):
                                    c = g * 4 + c4
                                    tr(ps[pb][:tt, c4 * 128:(c4 + 1) * 128], yT[:, c, tl * tt:(tl + 1) * tt], ident[:],
                                       [B_y, B_const], [P[pb]])
                                if g == 0:
                                    vcopy("dve", stg[k2][:tt, 0:512], ps[pb][:tt, :], [P[pb]], [B_st[k2]])
                                else:
                                    tk.op("act", lambda e: e.copy(stg[k2][:tt, 512:1024], ps[pb][:tt, :]), [P[pb]], [B_st[k2]])
                            tk.dma("sp", outs["y"][ti * tt:(ti + 1) * tt, :], stg[k2][:tt, :], R=[B_st[k2]])
                    tk.barrier()

            nph = dbg_nph
            if nph >= 1:
                ffn(0, 0)
            if nph >= 2:
                with ExitStack() as sA:
                    oTA = sb(sA, "oTA", [128, 4, T], BF16)
                    B_oTA = Buf()
                    phase_A(oTA, B_oTA)
                    if nph >= 3:
                        phase_B(oTA, B_oTA)
            if nph >= 4:
                ffn(0, 1)
            if nph >= 5:
                ffn(1, 0)
            if nph >= 6:
                phase_C()
            if nph >= 7:
                ffn(1, 1)
            final()

        for sp_ in range(2):
            if dbg_pass is not None and sp_ not in dbg_pass:
                continue
            seq_pass(sp_, SEQ, I["x_p"][sp_], lambda ti: cs_pt[:, ti, :], None,
                     dict(y=O["y_p"][sp_], ak=O["ak_p"][sp_], av=O["av_p"][sp_], lat=O["lat_p"][sp_],
                          kr=O["kr_p"][sp_], ck=O["ck_p"][sp_], cv=O["cv_p"][sp_]))
        if dbg_pass is None or 2 in dbg_pass:
            seq_pass(2, DEC, I["x_s"], lambda ti: cs_st[:, 0, :],
                     dict(a_k=I["ca_k"], a_v=I["ca_v"], lat=I["c_lat"], kr=I["c_kr"], c_k=I["cc_k"], c_v=I["cc_v"]),
                     dict(y=O["y_s"], ak=O["ak_s"], av=O["av_s"], lat=O["lat_s"], kr=O["kr_s"], ck=O["ck_s"], cv=O["cv_s"]))
        tk.barrier()
    return nc


_NC = None
_DBG = ()
_NCORES = 8


def _rope_table(pos):
    half = 32
    inv = (1.0 / (10000.0 ** (np.arange(half, dtype=np.float32) * np.float32(2.0 / 64)))).astype(np.float32)
    ang = pos.astype(np.float32)[:, None] * inv[None, :]
    return np.concatenate([np.cos(ang), np.sin(ang)], axis=1).astype(np.float32)


def kernel(**inp):
    global _NC
    if _NC is None:
        _NC = _build(*_DBG)
    f = lambda a: np.ascontiguousarray(np.asarray(a, dtype=np.float32))
    shared = dict(
        ada_w=f(inp["ada_w"]), ada_b=f(inp["ada_b"]), norm_g=f(inp["norm_g"]),
        ffn_w_in=f(inp["ffn_w_in"]), ffn_w_out=f(inp["ffn_w_out"]),
        ab_w_in=f(inp["ab_w_in"][0]), a_lambda=f(inp["a_lambda"][0]).reshape(256), a_subln_g=f(inp["a_subln_g"][0]),
        mla_q_norm_g=f(inp["mla_q_norm_g"][0]), mla_w_uq=f(inp["mla_w_uq"][0]),
        mla_kv_norm_g=f(inp["mla_kv_norm_g"][0]), mla_w_ukv=f(inp["mla_w_ukv"][0]),
        ab_w_out=f(inp["ab_w_out"][0]), c_w_in=f(inp["c_w_in"][0]), c_rel_bias=f(inp["c_rel_bias"][0]),
        c_w_out=f(inp["c_w_out"][0]), final_norm_g=f(inp["final_norm_g"]),
        ident=np.eye(128, dtype=np.float32), antiid=np.ascontiguousarray(np.eye(128, dtype=np.float32)[::-1]),
        cs_p=_rope_table(np.arange(SEQ)), cs_s=_rope_table(PAST + np.arange(DEC)),
    )
    xp, xs = f(inp["x_prompt"]), f(inp["x_sample"])
    cp, cs = f(inp["c_prompt"]), f(inp["c_sample"])
    in_maps = []
    for k in range(8):
        m = dict(shared)
        m["x_p"] = xp[2 * k:2 * k + 2]
        m["x_s"] = xs[k]
        m["c_all"] = np.ascontiguousarray(np.concatenate([cp[2 * k:2 * k + 2], cs[k:k + 1]], axis=0))
        m["ca_k"] = f(inp["cache_a_k"][0, k]).reshape(PAST, 512)
        m["ca_v"] = f(inp["cache_a_v"][0, k]).reshape(PAST, 512)
        m["c_lat"] = f(inp["cache_mla_latent"][0, k])
        m["c_kr"] = f(inp["cache_mla_krope"][0, k])
        m["cc_k"] = f(inp["cache_c_k"][0, k]).reshape(CPAST, D)
        m["cc_v"] = f(inp["cache_c_v"][0, k]).reshape(CPAST, D)
        in_maps.append(m)
    ncore = _NCORES
    res = run_bass_kernel_spmd(_NC, in_maps[:ncore], core_ids=list(range(ncore)))
    R = list(res.results) + [res.results[0]] * (8 - ncore)

    def cat(name, shape):
        return np.concatenate([np.asarray(R[k][name], dtype=np.float32).reshape(shape) for k in range(8)], axis=0)

    y_p = cat("y_p", (2, SEQ, D))
    y_s = cat("y_s", (1, DEC, D))
    ak_p = cat("ak_p", (2, SEQ, 4, 2, 64))[None]
    ak_s = cat("ak_s", (1, DEC, 4, 2, 64))[None]
    av_p = cat("av_p", (2, SEQ, 4, 128))[None]
    av_s = cat("av_s", (1, DEC, 4, 128))[None]
    lat_p = cat("lat_p", (2, SEQ, 256))[None]
    lat_s = cat("lat_s", (1, DEC, 256))[None]
    kr_p = cat("kr_p", (2, SEQ, 64))[None]
    kr_s = cat("kr_s", (1, DEC, 64))[None]
    ck_p = cat("ck_p", (2, CPAST, 16, 64))[None]
    ck_s = cat("ck_s", (1, CPAST, 16, 64))[None]
    cv_p = cat("cv_p", (2, CPAST, 16, 64))[None]
    cv_s = cat("cv_s", (1, CPAST, 16, 64))[None]
    return (y_p, y_s, ak_p, ak_s, av_p, av_s, lat_p, lat_s, kr_p, kr_s, ck_p, ck_s, cv_p, cv_s)
```

```python
import math
import os
from contextlib import ExitStack
KSKIP = os.environ.get("KSKIP", "").split(",")
ATT_NB = int(os.environ.get("ATT_NB", "2"))
SQ_ENG = os.environ.get("SQ_ENG", "act")

import numpy as np
import concourse.bass as bass
import concourse.mybir as mybir
from concourse.bass_utils import run_bass_kernel_spmd

F32 = mybir.dt.float32
BF16 = mybir.dt.bfloat16
AF = mybir.ActivationFunctionType
ALU = mybir.AluOpType
AX = mybir.AxisListType

D = 1024
NCH = 8
SEQ = 2048
DEC = 16
PAST = 1024
CPAST = 512
DFF = 2816
EPS = 1e-5
A_SCALE = 64 ** -0.5
MLA_SCALE = 192 ** -0.5
C_SCALE = 64 ** -0.5
LAM_INIT0 = 0.8 - 0.6 * math.exp(-0.3 * 0)
NDS = 40


class Buf:
    __slots__ = ("w", "r", "excl")

    def __init__(self, excl=False):
        self.w = None
        self.r = {}
        self.excl = excl


class Trk:
    def __init__(self, nc, es):
        self.nc = nc
        self.E = {}
        for n, o in (("pe", nc.tensor), ("dve", nc.vector), ("act", nc.scalar),
                     ("pool", nc.gpsimd), ("sp", nc.sync)):
            self.E[n] = dict(o=o, sem=es.enter_context(nc.semaphore("s_" + n)), cnt=0, seen={})
        self.ds = [dict(sem=es.enter_context(nc.semaphore("d%d" % i)), tot=0) for i in range(NDS)]
        self.dnext = 0
        self.dnext_pool = 0

    def wait(self, en, ev):
        if ev is None:
            return
        key, val = ev
        if key == en and en == "pe":
            return
        e = self.E[en]
        if e["seen"].get(key, 0) >= val:
            return
        sem = self.E[key]["sem"] if isinstance(key, str) else self.ds[key]["sem"]
        e["o"].wait_ge(sem, val)
        e["seen"][key] = val

    def _deps(self, en, R, W):
        for b in R:
            self.wait(en, b.w)
        for b in W:
            self.wait(en, b.w)
            for ev in list(b.r.values()):
                self.wait(en, ev)

    def _mark(self, ev, R, W):
        for b in R:
            b.r[ev[0]] = ev
        for b in W:
            b.w = ev
            b.r = {}

    @staticmethod
    def _split(R, W):
        R2 = [b for b in R if not b.excl]
        W2 = list(W) + [b for b in R if b.excl]
        return R2, W2

    def op(self, en, fn, R=(), W=()):
        R, W = self._split(R, W)
        self._deps(en, R, W)
        e = self.E[en]
        ins = fn(e["o"])
        e["cnt"] += 1
        ins.then_inc(e["sem"], 1)
        self._mark((en, e["cnt"]), R, W)
        return ins

    def dma(self, en, out, in_, R=(), W=(), **kw):
        R, W = self._split(R, W)
        half = NDS // 2
        if en == "pool":
            i = half + self.dnext_pool
            self.dnext_pool = (self.dnext_pool + 1) % half
        else:
            i = self.dnext
            self.dnext = (self.dnext + 1) % half
        d = self.ds[i]
        if d["tot"] > 0:
            self.wait(en, (i, d["tot"]))
        self._deps(en, R, W)
        ins = self.E[en]["o"].dma_start(out=out, in_=in_, **kw)
        d["tot"] += 16
        ins.then_inc(d["sem"], 16)
        self._mark((i, d["tot"]), R, W)

    def barrier(self):
        for en in self.E:
            for k2, e2 in self.E.items():
                if k2 != en and e2["cnt"] > 0:
                    self.wait(en, (k2, e2["cnt"]))
            for i, d in enumerate(self.ds):
                if d["tot"] > 0:
                    self.wait(en, (i, d["tot"]))


def _build(dbg_pass=None, dbg_nph=99):
    nc = bass.Bass("TRN2", target_bir_lowering=False)

    def din(name, shape):
        return nc.dram_tensor(name, list(shape), F32, kind="ExternalInput").ap()

    def dout(name, shape):
        return nc.dram_tensor(name, list(shape), F32, kind="ExternalOutput").ap()

    I = {}
    for name, shape in [
        ("x_p", (2, SEQ, D)), ("x_s", (DEC, D)), ("c_all", (3, D)),
        ("ca_k", (PAST, 512)), ("ca_v", (PAST, 512)), ("c_lat", (PAST, 256)), ("c_kr", (PAST, 64)),
        ("cc_k", (CPAST, D)), ("cc_v", (CPAST, D)),
        ("ada_w", (2, D, 9 * D)), ("ada_b", (2, 9 * D)), ("norm_g", (2, 3, D)),
        ("ffn_w_in", (2, 2, D, 2 * DFF)), ("ffn_w_out", (2, 2, DFF, D)),
        ("ab_w_in", (D, 2240)), ("a_lambda", (256,)), ("a_subln_g", (128,)),
        ("mla_q_norm_g", (384,)), ("mla_w_uq", (384, 768)), ("mla_kv_norm_g", (256,)),
        ("mla_w_ukv", (256, 1024)), ("ab_w_out", (D, D)), ("c_w_in", (D, 3 * D)),
        ("c_rel_bias", (16, 257)), ("c_w_out", (D, D)), ("final_norm_g", (D,)),
        ("ident", (128, 128)), ("antiid", (128, 128)),
        ("cs_p", (SEQ, 64)), ("cs_s", (DEC, 64)),
    ]:
        I[name] = din(name, shape)
    O = {}
    for name, shape in [
        ("y_p", (2, SEQ, D)), ("y_s", (DEC, D)),
        ("ak_p", (2, SEQ, 512)), ("ak_s", (DEC, 512)), ("av_p", (2, SEQ, 512)), ("av_s", (DEC, 512)),
        ("lat_p", (2, SEQ, 256)), ("lat_s", (DEC, 256)), ("kr_p", (2, SEQ, 64)), ("kr_s", (DEC, 64)),
        ("ck_p", (2, CPAST, D)), ("ck_s", (CPAST, D)), ("cv_p", (2, CPAST, D)), ("cv_s", (CPAST, D)),
    ]:
        O[name] = dout(name, shape)
    rbp = nc.dram_tensor("rbp_scratch", [16, 384], F32, kind="Internal").ap()

    with ExitStack() as es:
        tk = Trk(nc, es)

        uid = [0]

        def sb(scope, name, shape, dt=F32):
            uid[0] += 1
            return scope.enter_context(nc.sbuf_tensor("%s_%d" % (name, uid[0]), list(shape), dt))

        def mm(out, lhsT, rhs, start, stop, R, W, sgc=False):
            if sgc:
                tk.op("pe", lambda e: e.matmul(out, lhsT, rhs, start=start, stop=stop, skip_group_check=True), R, W)
            else:
                tk.op("pe", lambda e: e.matmul(out, lhsT, rhs, start=start, stop=stop), R, W)

        def tr(out, in_, idn, R, W):
            tk.op("pe", lambda e: e.transpose(out, in_, idn), R, W)

        def act(out, in_, func, R, W, bias=None, scale=1.0):
            if bias is None:
                tk.op("act", lambda e: e.activation(out, in_, func, scale=scale), R, W)
            else:
                tk.op("act", lambda e: e.activation(out, in_, func, bias=bias, scale=scale), R, W)

        def vcopy(en, out, in_, R, W):
            tk.op(en, lambda e: e.tensor_copy(out, in_), R, W)

        def tt_(en, out, a, b, op, R, W):
            tk.op(en, lambda e: e.tensor_tensor(out, a, b, op=op), R, W)

        def ts_(en, out, a, s1, s2, op0, op1, R, W):
            if op1 is None:
                tk.op(en, lambda e: e.tensor_scalar(out, a, s1, None, op0=op0), R, W)
            else:
                tk.op(en, lambda e: e.tensor_scalar(out, a, s1, s2, op0=op0, op1=op1), R, W)

        def stt(en, out, a, s, b, op0, op1, R, W):
            tk.op(en, lambda e: e.scalar_tensor_tensor(out, a, s, b, op0=op0, op1=op1), R, W)

        def recip(out, in_, R, W):
            tk.op("dve", lambda e: e.reciprocal(out, in_), R, W)

        def rpow(out, in_, R, W, power=1.0, bias=None, scale=1.0):
            if bias is None:
                tk.op("act", lambda e: e.activation(out, in_, AF.Ln, scale=scale), R, W)
            else:
                tk.op("act", lambda e: e.activation(out, in_, AF.Ln, bias=bias, scale=scale), R + [B_const], W)
            tk.op("act", lambda e: e.activation(out, out, AF.Exp, scale=-power), W, W)

        def mset(en, ap, val, W):
            tk.op(en, lambda e: e.memset(ap, val), (), W)

        top = es
        ident = sb(top, "ident", [128, 128])
        identb = sb(top, "identb", [128, 128], BF16)
        antiid = sb(top, "antiid", [128, 128])
        ones_d = sb(top, "ones_d", [128, 128], BF16)
        ones_e = sb(top, "ones_e", [128, 128], BF16)
        ones_1 = sb(top, "ones_1", [128, 128], BF16)
        epsc = sb(top, "epsc", [128, 1])
        modT = sb(top, "modT", [128, 144, 3])
        acol = sb(top, "acol", [128, 48, 3])
        gcol = sb(top, "gcol", [128, 48, 3])
        gT = sb(top, "gT", [128, 48])
        fgT = sb(top, "fgT", [128, 8])
        adabT = sb(top, "adabT", [128, 144])
        negl = sb(top, "negl", [128, 1])
        gsub = sb(top, "gsub", [128, 1])
        rb0 = sb(top, "rb0", [128, 16])
        gq_b = sb(top, "gq_b", [128, 384])
        gkv_b = sb(top, "gkv_b", [128, 256])
        cs_pt = sb(top, "cs_pt", [128, 16, 64])
        cs_st = sb(top, "cs_st", [16, 1, 64])
        xT = sb(top, "xT", [128, NCH, SEQ])
        B_const = Buf()
        B_mod = Buf()
        B_x = [Buf() for _ in range(4)]

        ps = [es.enter_context(nc.psum_tensor("ps%d" % i, [128, 512], F32)) for i in range(8)]
        psb = ps[7][:, :].bitcast(BF16)
        P = [Buf(True) for _ in range(8)]
        PB = [P[7], P[7]]

        tk.dma("sp", ident[:], I["ident"], W=[B_const])
        tk.dma("sp", antiid[:], I["antiid"], W=[B_const])
        tk.dma("pool", identb[:], I["ident"], W=[B_const])
        stA = sb(top, "stA", [128, 128])
        stB = sb(top, "stB", [128, 128])
        cT = sb(top, "cT", [128, 8, 3])
        B_stg = Buf()
        mset("dve", stB[:], 0.0, [B_stg])
        adv = I["ada_b"].rearrange("l (q p) -> (l q) p", p=128)
        tk.dma("sp", stA[:], adv[0:128, :], W=[B_stg])
        tk.dma("sp", stB[0:16, :], adv[128:144, :], W=[B_stg])
        tk.dma("sp", stB[16:64, :], I["norm_g"].rearrange("l i (c p) -> (l i c) p", p=128), W=[B_stg])
        tk.dma("sp", stB[64:72, :], I["final_norm_g"].rearrange("(c p) -> c p", p=128), W=[B_stg])
        tk.dma("sp", stB[72:96, :], I["c_all"].rearrange("s (c p) -> (s c) p", p=128), W=[B_stg])
        tk.dma("sp", stB[96:97, :], I["a_subln_g"].rearrange("(o p) -> o p", o=1), W=[B_stg])
        tr(ps[0][:, 0:128], stA[:], ident[:], [B_stg, B_const], [P[0]])
        tr(ps[1][:, 0:128], stB[:], ident[:], [B_stg, B_const], [P[1]])
        vcopy("dve", adabT[:, 0:128], ps[0][:, 0:128], [P[0]], [B_const])
        vcopy("dve", adabT[:, 128:144], ps[1][:, 0:16], [P[1]], [B_const])
        vcopy("dve", gT[:], ps[1][:, 16:64], [P[1]], [B_const])
        vcopy("dve", fgT[:], ps[1][:, 64:72], [P[1]], [B_const])
        vcopy("dve", cT[:], ps[1][:, 72:96].rearrange("p (s c) -> p c s", s=3), [P[1]], [B_mod])
        vcopy("dve", gsub[:], ps[1][:, 96:97], [P[1]], [B_const])
        tk.dma("sp", gq_b[:], I["mla_q_norm_g"].partition_broadcast(128), W=[B_const])
        tk.dma("sp", gkv_b[:], I["mla_kv_norm_g"].partition_broadcast(128), W=[B_const])
        tk.dma("sp", cs_pt[:], I["cs_p"].rearrange("(t p) f -> p t f", p=128), W=[B_const])
        tk.dma("sp", cs_st[:, 0, :], I["cs_s"], W=[B_const])
        mset("dve", ones_d[:], 1.0 / 1024.0, [B_const])
        mset("dve", ones_e[:], 1.0 / 128.0, [B_const])
        mset("dve", ones_1[:], 1.0, [B_const])
        mset("dve", epsc[:], EPS, [B_const])
        ts_("dve", gsub[:], gsub[:], 1.0 - LAM_INIT0, None, ALU.mult, None, [B_const], [B_const])

        with ExitStack() as sc:
            lamb = sb(sc, "lamb", [128, 256])
            lt = sb(sc, "lt", [128, 128])
            l2 = sb(sc, "l2", [128, 2])
            tk.dma("sp", lamb[:], I["a_lambda"].partition_broadcast(128), W=[B_const])
            tt_("dve", lt[:, 0:64], lamb[:, 0:64], lamb[:, 64:128], ALU.mult, [B_const], [B_const])
            tt_("dve", lt[:, 64:128], lamb[:, 128:192], lamb[:, 192:256], ALU.mult, [B_const], [B_const])
            tk.op("dve", lambda e: e.reduce_sum(l2[:], lt[:].rearrange("p (a b) -> p a b", a=2), axis=AX.X),
                  [B_const], [B_const])
            act(l2[:], l2[:], AF.Exp, [B_const], [B_const])
            tt_("dve", negl[:], l2[:, 1:2], l2[:, 0:1], ALU.subtract, [B_const], [B_const])
            ts_("dve", negl[:], negl[:], -LAM_INIT0, None, ALU.add, None, [B_const], [B_const])

            B_rbp, B_rbs = Buf(), Buf()
            rbs = sb(sc, "rbs", [16, 384])
            dg = sb(sc, "dg", [16, 16])
            ones_f = sb(sc, "ones_f", [16, 128])
            tk.dma("sp", rbs[:, 127:384], I["c_rel_bias"], W=[B_rbs])
            act(rbs[:, 0:127], rbs[:, 127:254], AF.Identity, [B_rbs], [B_rbs], bias=rbs[:, 127:128], scale=0.0)
            tk.dma("sp", rbp[:, :], rbs[:], R=[B_rbs], W=[B_rbp])
            mset("dve", ones_f[:], 1.0, [B_rbs])
            ts_("dve", dg[:], ident[0:16, 0:16], rbs[:, 127:128], None, ALU.mult, None, [B_rbs, B_const], [B_rbs])
            mm(ps[2][:, 0:16], ones_f[:], dg[:], True, True, [B_rbs], [P[2]])
            vcopy("dve", rb0[:], ps[2][:, 0:16], [P[2]], [B_const])
            scT = sb(sc, "scT", [128, 8, 3], BF16)
            wad = [sb(sc, "wad%d" % i, [128, 8, 1024], BF16) for i in range(2)]
            Bw = [Buf(), Buf()]
            act(scT[:], cT[:], AF.Silu, [B_mod], [B_mod])
            for l in range(2):
                for j in range(9):
                    g = l * 9 + j
                    sl = g % 2
                    tk.dma("pool", wad[sl][:],
                           I["ada_w"][l, :, j * 1024:(j + 1) * 1024].rearrange("(kc p) n -> p kc n", p=128),
                           W=[Bw[sl]])
                    pb = P[4 + g % 2]
                    pt = ps[4 + g % 2]
                    for oc in range(8):
                        for kc in range(8):
                            mm(pt[:, oc * 3:oc * 3 + 3], wad[sl][:, kc, oc * 128:(oc + 1) * 128], scT[:, kc, :],
                               kc == 0, kc == 7, [Bw[sl], B_mod], [pb])
                    tt_("dve", modT[:, g * 8:(g + 1) * 8, :],
                        pt[:, 0:24].rearrange("p (a s) -> p a s", s=3),
                        adabT[:, g * 8:(g + 1) * 8].unsqueeze(2).to_broadcast([128, 8, 3]),
                        ALU.add, [pb, B_const], [B_mod])
            for l in range(2):
                for i in range(3):
                    q = l * 3 + i
                    sc_ap = modT[:, (l * 9 + 3 * i + 1) * 8:(l * 9 + 3 * i + 2) * 8, :]
                    gt_ap = modT[:, (l * 9 + 3 * i + 2) * 8:(l * 9 + 3 * i + 3) * 8, :]
                    stt("dve", acol[:, q * 8:(q + 1) * 8, :], sc_ap, 1.0,
                        gT[:, q * 8:(q + 1) * 8].unsqueeze(2).to_broadcast([128, 8, 3]),
                        ALU.add, ALU.mult, [B_mod, B_const], [B_mod])
                    ts_("dve", gcol[:, q * 8:(q + 1) * 8, :], gt_ap, 1.0 if i == 1 else 0.5, None,
                        ALU.mult, None, [B_mod], [B_mod])
            tk.barrier()

        def shift_col(l, i, c, s):
            return modT[:, (l * 9 + 3 * i) * 8 + c, s:s + 1]

        def a_col(l, i, c, s):
            return acol[:, (l * 3 + i) * 8 + c, s:s + 1]

        def g_col(l, i, c, s):
            return gcol[:, (l * 3 + i) * 8 + c, s:s + 1]

        def seq_pass(s, T, x_src, cs_tile, caches, outs):
            tt = min(128, T)
            ntile = T // tt
            bs = min(512, T)
            nblk = T // bs
            tpb = bs // tt
            Pa = PAST if caches else 0
            Pc = CPAST if caches else 0
            is_p = caches is None

            def blk(b):
                return slice(b * bs, (b + 1) * bs)

            with ExitStack() as sc:
                xin = [sb(sc, "xin%d" % i, [128, D]) for i in range(2)]
                Bxi = [Buf(), Buf()]
                for i in range(ntile):
                    sl = i % 2
                    tk.dma("sp", xin[sl][:tt, :], x_src[i * tt:(i + 1) * tt, :], W=[Bxi[sl]])
                    for g in range(2):
                        pb, pt = P[(i * 2 + g) % 4], ps[(i * 2 + g) % 4]
                        for c4 in range(4):
                            c = g * 4 + c4
                            tr(pt[:, c4 * tt:(c4 + 1) * tt], xin[sl][:tt, c * 128:(c + 1) * 128], ident[:tt, :tt],
                               [Bxi[sl], B_const], [pb])
                        dst = xT[:, g * 4:(g + 1) * 4, i * tt:(i + 1) * tt]
                        srcv = pt[:, 0:4 * tt].rearrange("p (c t) -> p c t", c=4)
                        if g == 0:
                            vcopy("dve", dst, srcv, [pb], [B_x[(i * tt) // 512]])
                        else:
                            tk.op("act", lambda e: e.copy(dst, srcv), [pb], [B_x[(i * tt) // 512]])
                tk.barrier()

            def modulate(sc_, l, i, b, hT_ap, B_h, scr):
                sq, rstd, tmp, Bs = scr
                for c in range(8):
                    if SQ_ENG == "act":
                        act(sq[:, c % 2, :bs], xT[:, c, blk(b)], AF.Square, [B_x[b]], [Bs[0 + c % 2]])
                    else:
                        tt_(SQ_ENG, sq[:, c % 2, :bs], xT[:, c, blk(b)], xT[:, c, blk(b)], ALU.mult, [B_x[b]], [Bs[0 + c % 2]])
                    mm(ps[6][:, :bs], ones_d[:], sq[:, c % 2, :bs], c == 0, c == 7, [Bs[c % 2], B_const], [P[6]])
                rpow(rstd[:, :bs], ps[6][:, :bs], [P[6]], [Bs[2]], power=0.5, bias=epsc[:])
                for c in range(8):
                    stt("dve", tmp[:, c % 2, :bs], xT[:, c, blk(b)], a_col(l, i, c, s), rstd[:, :bs],
                        ALU.mult, ALU.mult, [B_x[b], B_mod, Bs[2]], [Bs[3 + c % 2]])
                    act(hT_ap[:, c, :bs], tmp[:, c % 2, :bs], AF.Identity, [Bs[3 + c % 2], B_mod], [B_h],
                        bias=shift_col(l, i, c, s))

            def mod_scratch(sc_):
                sq = sb(sc_, "m_sq", [128, 2, 512], BF16)
                rstd = sb(sc_, "m_rstd", [128, 512])
                tmp = sb(sc_, "m_tmp", [128, 2, 512])
                return (sq, rstd, tmp, [Buf() for _ in range(5)])

            def ffn(l, fi):
                i = 0 if fi == 0 else 2
                pieces = [(0, 4), (4, 4), (8, 4), (12, 4), (16, 3), (19, 3)]
                w_in = I["ffn_w_in"][l, fi]
                w_out = I["ffn_w_out"][l, fi]
                with ExitStack() as sc_:
                    hT = sb(sc_, "f_hT", [128, 8, T], BF16)
                    Bh = [Buf() for _ in range(nblk)]
                    actT = [sb(sc_, "f_act%d" % k, [128, 4, T], BF16) for k in range(2)]
                    Ba = [[Buf() for _ in range(nblk)] for k in range(2)]
                    wg = [sb(sc_, "f_wg%d" % k, [128, 8, 512], BF16) for k in range(2)]
                    wu = [sb(sc_, "f_wu%d" % k, [128, 8, 512], BF16) for k in range(2)]
                    wo = [sb(sc_, "f_wo%d" % k, [128, 4, D], BF16) for k in range(2)]
                    Bwg = [Buf(), Buf()]
                    Bwu = [Buf(), Buf()]
                    Bwo = [Buf(), Buf()]
                    sg = sb(sc_, "f_sg", [128, 2, 512])
                    Bsg = [Buf(), Buf()]
                    scr = mod_scratch(sc_)

                    def load(pi):
                        c0, n = pieces[pi]
                        k = pi % 2
                        tk.dma("pool", wg[k][:, :, :n * 128],
                               w_in[:, c0 * 128:(c0 + n) * 128].rearrange("(kc p) n -> p kc n", p=128), W=[Bwg[k]])
                        tk.dma("pool", wu[k][:, :, :n * 128],
                               w_in[:, DFF + c0 * 128:DFF + (c0 + n) * 128].rearrange("(kc p) n -> p kc n", p=128),
                               W=[Bwu[k]])
                        tk.dma("pool", wo[k][:, :n, :],
                               w_out[c0 * 128:(c0 + n) * 128, :].rearrange("(j p) f -> p j f", p=128), W=[Bwo[k]])

                    load(0)
                    modulate(sc_, l, i, 0, hT[:, :, blk(0)], Bh[0], scr)
                    cnt = 0
                    for pi, (c0, n) in enumerate(pieces):
                        k = pi % 2
                        if pi + 1 < len(pieces):
                            load(pi + 1)
                        for b in range(nblk):
                            if pi == 0 and b + 1 < nblk:
                                modulate(sc_, l, i, b + 1, hT[:, :, blk(b + 1)], Bh[b + 1], scr)
                            for j in range(n):
                                gb, ub = cnt % 2, 2 + cnt % 2
                                for kc in range(8):
                                    mm(ps[gb][:, :bs], wg[k][:, kc, j * 128:(j + 1) * 128], hT[:, kc, blk(b)],
                                       kc == 0, kc == 7, [Bwg[k], Bh[b]], [P[gb]])
                                for kc in range(8):
                                    mm(ps[ub][:, :bs], wu[k][:, kc, j * 128:(j + 1) * 128], hT[:, kc, blk(b)],
                                       kc == 0, kc == 7, [Bwu[k], Bh[b]], [P[ub]])
                                act(sg[:, cnt % 2, :bs], ps[gb][:, :bs], AF.Silu, [P[gb]], [Bsg[cnt % 2]])
                                tt_("dve", actT[k][:, j, blk(b)], ps[ub][:, :bs], sg[:, cnt % 2, :bs], ALU.mult,
                                    [P[ub], Bsg[cnt % 2]], [Ba[k][b]])
                                cnt += 1
                        for b in range(nblk):
                            for fo in range(8):
                                yb = (4, 5, 7)[fo % 3]
                                for j in range(n):
                                    mm(ps[yb][:, :bs], wo[k][:, j, fo * 128:(fo + 1) * 128], actT[k][:, j, blk(b)],
                                       j == 0, j == n - 1, [Bwo[k], Ba[k][b]], [P[yb]])
                                stt("dve", xT[:, fo, blk(b)], ps[yb][:, :bs], g_col(l, i, fo, s), xT[:, fo, blk(b)],
                                    ALU.mult, ALU.add, [P[yb], B_mod], [B_x[b]])
                    tk.barrier()

            def rope(en, dst, src, nmap, ti, tmpa, R, W, Bt):
                cs = cs_tile(ti)
                cosb = cs[:tt, 0:32].unsqueeze(1).to_broadcast([tt, nmap, 32])
                sinb = cs[:tt, 32:64].unsqueeze(1).to_broadcast([tt, nmap, 32])
                sv = src.rearrange("p (m two f) -> p m two f", two=2, f=32)
                dv = dst.rearrange("p (m two f) -> p m two f", two=2, f=32)
                x1, x2 = sv[:, :, 0, :], sv[:, :, 1, :]
                t = tmpa[:tt, :nmap * 64].rearrange("p (k m f) -> p k m f", k=2, f=32)
                tt_(en, t[:, 0], x1, cosb, ALU.mult, R + [B_const], [Bt])
                tt_(en, t[:, 1], x2, sinb, ALU.mult, R + [B_const], [Bt])
                tt_(en, dv[:, :, 0, :], t[:, 0], t[:, 1], ALU.subtract, [Bt], W)
                tt_(en, t[:, 0], x2, cosb, ALU.mult, R + [B_const], [Bt])
                tt_(en, t[:, 1], x1, sinb, ALU.mult, R + [B_const], [Bt])
                tt_(en, dv[:, :, 1, :], t[:, 0], t[:, 1], ALU.add, [Bt], W)

            def key_tiles(Pp):
                kts = [(j * 128, 128) for j in range(Pp // 128)]
                kts += [(Pp + j * tt, tt) for j in range(ntile)]
                return kts

            def attn_qblock(b, score_parts, v_ap_fn, kts, scale, o_bank, d_bank, Rk, pT, BpT, cntr):
                npast = len(kts) - ntile
                q0t = b * tpb
                vis = [k for k in range(len(kts)) if (k < npast or (k - npast) <= q0t + tpb - 1)]
                info = {}
                NB = ATT_NB
                SB = [0, 1, 6, 7]

                def stage_s(n_):
                    k = vis[n_]
                    koff, ks = kts[k]
                    r = 0
                    if is_p and k - npast > q0t:
                        r = k - npast - q0t
                    cs_ = slice(r * tt, bs)
                    slot = cntr[0] % (2 * NB)
                    cntr[0] += 1
                    sbk = SB[slot]
                    info[n_] = (k, ks, cs_, slot)
                    for pi_, (kT_ap, qT_ap) in enumerate(score_parts):
                        mm(ps[sbk][:ks, cs_], kT_ap[:, koff:koff + ks], qT_ap[:, b * bs + r * tt:(b + 1) * bs],
                           pi_ == 0, pi_ == len(score_parts) - 1, Rk, [P[sbk]])

                def stage_e(n_):
                    k, ks, cs_, slot = info[n_]
                    sbk = SB[slot]
                    act(pT[slot][:ks, cs_], ps[sbk][:ks, cs_], AF.Exp, [P[sbk]], [BpT[slot]], scale=scale)
                    if is_p and k - npast >= q0t:
                        r = cs_.start // tt
                        mset("pool", pT[slot][64:128, r * tt:r * tt + 64], 0.0, [BpT[slot]])

                def stage_do(n_):
                    k, ks, cs_, slot = info[n_]
                    mm(ps[d_bank][:, cs_], ones_1[:ks, :], pT[slot][:ks, cs_], n_ == 0, n_ == len(vis) - 1,
                       [BpT[slot], B_const], [P[d_bank]])
                    mm(ps[o_bank][:, cs_], v_ap_fn(k, ks), pT[slot][:ks, cs_], n_ == 0, n_ == len(vis) - 1,
                       [BpT[slot]] + Rk, [P[o_bank]])

                batches = [list(range(i0, min(i0 + NB, len(vis)))) for i0 in range(0, len(vis), NB)]

                def sb_(j):
                    for n_ in batches[j]:
                        stage_s(n_)
                    for n_ in batches[j]:
                        stage_e(n_)

                sb_(0)
                for j in range(len(batches)):
                    if j + 1 < len(batches):
                        sb_(j + 1)
                    for n_ in batches[j]:
                        stage_do(n_)

            def phase_A(oTA, B_oTA):
                l, i = 0, 1
                kts = key_tiles(Pa)
                nkt = len(kts)
                with ExitStack() as sc_:
                    aqT = sb(sc_, "a_qT", [128, 4, T], BF16)
                    akT = sb(sc_, "a_kT", [128, 4, Pa + T], BF16)
                    av = sb(sc_, "a_v", [128, nkt, 512], BF16)
                    B_q, B_k, B_v = Buf(), Buf(), Buf()
                    with ExitStack() as s2:
                        wA = sb(s2, "a_w", [128, 8, 1536], BF16)
                        B_w = Buf()
                        tk.dma("pool", wA[:, :, 0:768], I["ab_w_in"][:, 0:768].rearrange("(kc p) n -> p kc n", p=128),
                               W=[B_w])
                        tk.dma("pool", wA[:, :, 768:1536],
                               I["ab_w_in"][:, 768:1536].rearrange("(kc p) n -> p kc n", p=128), W=[B_w])
                        hTb = sb(s2, "a_hT", [128, 8, 512], BF16)
                        B_h = Buf()
                        scr = mod_scratch(s2)
                        raw = sb(s2, "a_raw", [128, 1024])
                        B_raw = Buf()
                        stg_v = [sb(s2, "a_sv%d" % k, [128, 512]) for k in range(2)]
                        stg_k = [sb(s2, "a_sk%d" % k, [128, 512]) for k in range(2)]
                        B_sv, B_sk = [Buf(), Buf()], [Buf(), Buf()]
                        qb2 = [sb(s2, "a_qb%d" % k, [128, 512], BF16) for k in range(2)]
                        kb2 = [sb(s2, "a_kb%d" % k, [128, 512], BF16) for k in range(2)]
                        B_qb2, B_kb2 = [Buf(), Buf()], [Buf(), Buf()]
                        tmq = sb(s2, "a_tmq", [128, 512])
                        tmk = sb(s2, "a_tmk", [128, 512])
                        B_tq, B_tk = Buf(), Buf()
                        if not is_p and "A_cache" not in KSKIP:
                            ckb = sb(s2, "a_ckb", [128, 8, 512], BF16)
                            B_ck = Buf()
                            tk.dma("pool", ckb[:], caches["a_k"].rearrange("(t p) f -> p t f", p=128), W=[B_ck])
                            tk.dma("pool", av[:, 0:8, :], caches["a_v"].rearrange("(t p) f -> p t f", p=128), W=[B_v])
                            for t_ in range(8):
                                pbi = t_ % 2
                                for h in range(4):
                                    tr(psb[:, pbi * 512 + h * 128: pbi * 512 + (h + 1) * 128],
                                       ckb[:, t_, h * 128:(h + 1) * 128], identb[:], [B_ck, B_const], [PB[pbi]])
                                vcopy("dve", akT[:, :, t_ * 128:(t_ + 1) * 128],
                                      psb[:, pbi * 512:(pbi + 1) * 512].rearrange("p (h t) -> p h t", h=4),
                                      [PB[pbi]], [B_k])
                        def stage1(ti):
                            b, tl = ti // tpb, ti % tpb
                            if tl == 0:
                                modulate(s2, l, i, b, hTb, B_h, scr)
                            tsl = slice(tl * tt, (tl + 1) * tt)
                            st = 3 * (ti % 2)
                            for n_ in range(3):
                                for kc in range(8):
                                    mm(ps[st + n_][:tt, :], hTb[:, kc, tsl], wA[:, kc, n_ * 512:(n_ + 1) * 512],
                                       kc == 0, kc == 7, [B_h, B_w], [P[st + n_]])
                            k2 = ti % 2
                            qb, kb, B_qb, B_kb = qb2[k2], kb2[k2], B_qb2[k2], B_kb2[k2]
                            kt_idx = Pa // 128 + ti
                            vcopy("dve", av[:tt, kt_idx, :], ps[st + 2][:tt, :], [P[st + 2]], [B_v])
                            tk.op("act", lambda e: e.copy(stg_v[k2][:tt, :], ps[st + 2][:tt, :]), [P[st + 2]], [B_sv[k2]])
                            tk.dma("sp", outs["av"][ti * tt:(ti + 1) * tt, :], stg_v[k2][:tt, :], R=[B_sv[k2]])
                            tk.op("act", lambda e: e.copy(raw[:tt, 0:512], ps[st][:tt, :]), [P[st]], [B_raw])
                            tk.op("act", lambda e: e.copy(raw[:tt, 512:1024], ps[st + 1][:tt, :]), [P[st + 1]], [B_raw])
                            rope("pool", qb[:tt, :], raw[:tt, 0:512], 8, ti, tmq, [B_raw], [B_qb], B_tq)
                            rope("dve", stg_k[k2][:tt, :], raw[:tt, 512:1024], 8, ti, tmk, [B_raw], [B_sk[k2]], B_tk)
                            tk.dma("sp", outs["ak"][ti * tt:(ti + 1) * tt, :], stg_k[k2][:tt, :], R=[B_sk[k2]])
                            vcopy("dve", kb[:tt, :], stg_k[k2][:tt, :], [B_sk[k2]], [B_kb])

                        def stage2(ti):
                            k2 = ti % 2
                            qb, kb, B_qb, B_kb = qb2[k2], kb2[k2], B_qb2[k2], B_kb2[k2]
                            for h in range(4):
                                tr(psb[:, h * 128:h * 128 + tt], qb[:tt, h * 128:(h + 1) * 128], identb[:tt, :tt],
                                   [B_qb, B_const], [PB[0]])
                            for h in range(4):
                                tr(psb[:, 512 + h * 128:512 + h * 128 + tt], kb[:tt, h * 128:(h + 1) * 128],
                                   identb[:tt, :tt], [B_kb, B_const], [PB[1]])
                            tk.op("act", lambda e: e.copy(
                                aqT[:, :, ti * tt:(ti + 1) * tt],
                                psb[:, 0:512].rearrange("p (h t) -> p h t", h=4)[:, :, :tt]), [PB[0]], [B_q])
                            vcopy("dve", akT[:, :, Pa + ti * tt:Pa + (ti + 1) * tt],
                                  psb[:, 512:1024].rearrange("p (h t) -> p h t", h=4)[:, :, :tt], [PB[1]], [B_k])

                        stage1(0)
                        for ti in range(ntile):
                            if ti + 1 < ntile:
                                stage1(ti + 1)
                            stage2(ti)
                        tk.barrier()
                    with ExitStack() as s2:
                        pT = [sb(s2, "a_pT%d" % k, [128, 512], BF16) for k in range(4)]
                        BpT = [Buf() for _ in range(4)]
                        on = [sb(s2, "a_on%d" % k, [128, 512]) for k in range(2)]
                        B_on = [Buf(), Buf()]
                        rr = sb(s2, "a_rr", [128, 512])
                        B_rr = Buf()
                        oa = sb(s2, "a_oa", [128, 512])
                        sq = sb(s2, "a_sq", [128, 512], BF16)
                        rs = sb(s2, "a_rs", [128, 512])
                        B_oa, B_sq, B_rs = Buf(), Buf(), Buf()
                        cntr = [0]
                        for b in range(nblk if "A_attn" not in KSKIP else 0):
                            for h in range(4):
                                for t_ in range(2):
                                    ob, db = 2 + t_, 4 + t_
                                    rows = slice(64 * t_, 64 * t_ + 64)
                                    attn_qblock(b, [(akT[rows, h, :], aqT[rows, h, :])],
                                                lambda k, ks: av[:ks, k, h * 128:(h + 1) * 128],
                                                kts, A_SCALE, ob, db, [B_q, B_k, B_v], pT, BpT, cntr)
                                    if "A_fin" in KSKIP:
                                        continue
                                    rpow(rr[:, :bs], ps[db][:, :bs], [P[db]], [B_rr])
                                    tt_("dve", on[t_][:, :bs], ps[ob][:, :bs], rr[:, :bs], ALU.mult,
                                        [P[ob], B_rr], [B_on[t_]])
                                if "A_fin" in KSKIP or "A_subln" in KSKIP:
                                    continue
                                stt("dve", oa[:, :bs], on[1][:, :bs], negl[:], on[0][:, :bs], ALU.mult, ALU.add,
                                    [B_on[0], B_on[1], B_const], [B_oa])
                                if "A_s1" in KSKIP:
                                    continue
                                act(sq[:, :bs], oa[:, :bs], AF.Square, [B_oa], [B_sq])
                                if "A_s2" in KSKIP:
                                    continue
                                mm(ps[5][:, :bs], ones_e[:], sq[:, :bs], True, True, [B_sq, B_const], [P[5]])
                                if "A_s3" in KSKIP:
                                    continue
                                rpow(rs[:, :bs], ps[5][:, :bs], [P[5]], [B_rs], power=0.5, bias=epsc[:])
                                if "A_s4" in KSKIP:
                                    continue
                                stt("dve", oTA[:, h, blk(b)], oa[:, :bs], gsub[:], rs[:, :bs], ALU.mult, ALU.mult,
                                    [B_oa, B_rs, B_const], [B_oTA])
                        tk.barrier()

            def phase_B(oTA, B_oTA):
                l, i = 0, 1
                kts = key_tiles(Pa)
                nkt = len(kts)
                with ExitStack() as sc_:
                    bqnT = sb(sc_, "b_qnT", [128, 4, T], BF16)
                    bqrT = sb(sc_, "b_qrT", [128, 2, T], BF16)
                    bknT = sb(sc_, "b_knT", [128, 4, Pa + T], BF16)
                    bv = sb(sc_, "b_v", [128, nkt, 512], BF16)
                    krT = sb(sc_, "b_krT", [128, Pa + T], BF16)
                    latT = sb(sc_, "b_latT", [128, 2, max(Pa, bs)], BF16)
                    B_qn, B_qr, B_kn, B_bv, B_kr, B_lat, B_wo = (Buf() for _ in range(7))
                    with ExitStack() as s2:
                        wB = sb(s2, "b_w", [128, 8, 704], BF16)
                        wuq_n = sb(s2, "b_wuqn", [128, 3, 512], BF16)
                        wuq_r = sb(s2, "b_wuqr", [128, 3, 256], BF16)
                        wkv_n = sb(s2, "b_wkvn", [128, 2, 512], BF16)
                        wkv_v = sb(s2, "b_wkvv", [128, 2, 512], BF16)
                        B_w = Buf()
                        tk.dma("pool", wB[:], I["ab_w_in"][:, 1536:2240].rearrange("(kc p) n -> p kc n", p=128), W=[B_w])
                        uq = I["mla_w_uq"].rearrange("(kc p) (h e) -> p kc h e", p=128, e=192)
                        ukv = I["mla_w_ukv"].rearrange("(kc p) (h e) -> p kc h e", p=128, e=256)
                        for kc in range(3):
                            tk.dma("pool", wuq_n[:, kc, :].rearrange("p (h e) -> p h e", e=128), uq[:, kc, :, 0:128], W=[B_w])
                            tk.dma("pool", wuq_r[:, kc, :].rearrange("p (h e) -> p h e", e=64), uq[:, kc, :, 128:192], W=[B_w])
                        for kc in range(2):
                            tk.dma("pool", wkv_n[:, kc, :].rearrange("p (h e) -> p h e", e=128), ukv[:, kc, :, 0:128], W=[B_w])
                            tk.dma("pool", wkv_v[:, kc, :].rearrange("p (h e) -> p h e", e=128), ukv[:, kc, :, 128:256], W=[B_w])
                        hTb = sb(s2, "b_hT", [128, 8, 512], BF16)
                        B_h = Buf()
                        scr = mod_scratch(s2)
                        sqs = sb(s2, "b_sqs", [128, 384])
                        ss = sb(s2, "b_ss", [128, 2])
                        B_sqs, B_ss = Buf(), Buf()
                        cqn2 = [sb(s2, "b_cqn%d" % k, [128, 384], BF16) for k in range(2)]
                        cqnT = sb(s2, "b_cqnT", [128, 3, 512], BF16)
                        B_cqn2, B_cqnT = [Buf(), Buf()], Buf()
                        stg_l = [sb(s2, "b_sl%d" % k, [128, 256]) for k in range(2)]
                        stg_r = [sb(s2, "b_sr%d" % k, [128, 64]) for k in range(2)]
                        B_sl, B_sr = [Buf(), Buf()], [Buf(), Buf()]
                        latb2 = [sb(s2, "b_latb%d" % k, [128, 256], BF16) for k in range(2)]
                        krb2 = [sb(s2, "b_krb%d" % k, [128, 128], BF16) for k in range(2)]
                        B_latb2, B_krb2 = [Buf(), Buf()], [Buf(), Buf()]
                        raw = sb(s2, "b_raw", [128, 256])
                        qrb = sb(s2, "b_qrb", [128, 256], BF16)
                        tmr = sb(s2, "b_tmr", [128, 256])
                        B_raw, B_qrb, B_tmr = Buf(), Buf(), Buf()

                        def kv_proj(col0, ncols, tiles, lat0):
                            for h in range(4):
                                pb = 4 + h % 2
                                for kc in range(2):
                                    mm(ps[pb][:, :ncols], wkv_n[:, kc, h * 128:(h + 1) * 128],
                                       latT[:, kc, lat0:lat0 + ncols], kc == 0, kc == 1, [B_w, B_lat], [P[pb]])
                                if h % 2 == 0:
                                    vcopy("dve", bknT[:, h, col0:col0 + ncols], ps[pb][:, :ncols], [P[pb]], [B_kn])
                                else:
                                    tk.op("act", lambda e: e.copy(bknT[:, h, col0:col0 + ncols], ps[pb][:, :ncols]),
                                          [P[pb]], [B_kn])
                            for (kidx, koff, ks) in tiles:
                                pb = 4 + kidx % 2
                                for kc in range(2):
                                    mm(ps[pb][:ks, :], latT[:, kc, koff:koff + ks], wkv_v[:, kc, :], kc == 0, kc == 1,
                                       [B_w, B_lat], [P[pb]])
                                vcopy("dve", bv[:ks, kidx, :], ps[pb][:ks, :], [P[pb]], [B_bv])

                        if not is_p:
                            clb = sb(s2, "b_clb", [128, 8, 256], BF16)
                            ckr = sb(s2, "b_ckr", [128, 8, 128], BF16)
                            B_cl = Buf()
                            tk.dma("pool", clb[:], caches["lat"].rearrange("(t p) f -> p t f", p=128), W=[B_cl])
                            tk.dma("pool", ckr[:, :, 0:64], caches["kr"].rearrange("(t p) f -> p t f", p=128), W=[B_cl])
                            tk.dma("pool", ckr[:, :, 64:128], caches["kr"].rearrange("(t p) f -> p t f", p=128), W=[B_cl])
                            for t_ in range(8):
                                pbi = t_ % 2
                                for c in range(2):
                                    tr(psb[:, pbi * 512 + c * 128:pbi * 512 + (c + 1) * 128], clb[:, t_, c * 128:(c + 1) * 128],
                                       identb[:], [B_cl, B_const], [PB[pbi]])
                                tr(psb[:, pbi * 512 + 256:pbi * 512 + 384], ckr[:, t_, :], identb[:], [B_cl, B_const], [PB[pbi]])
                                vcopy("dve", latT[:, :, t_ * 128:(t_ + 1) * 128],
                                      psb[:, pbi * 512:pbi * 512 + 256].rearrange("p (c t) -> p c t", c=2), [PB[pbi]], [B_lat])
                                vcopy("dve", krT[:, t_ * 128:(t_ + 1) * 128], psb[:, pbi * 512 + 256:pbi * 512 + 384],
                                      [PB[pbi]], [B_kr])
                            for hb in range(2):
                                kv_proj(hb * 512, 512, [(hb * 4 + j, hb * 512 + j * 128, 128) for j in range(4)], hb * 512)

                        def rms_tok(src_ps, n, g_b, k):
                            act(sqs[:tt, :n], src_ps, AF.Square, [P_src[0]], [B_sqs])
                            tk.op("dve", lambda e: e.reduce_sum(ss[:tt, k:k + 1], sqs[:tt, :n], axis=AX.X), [B_sqs], [B_ss])
                            rpow(ss[:tt, k:k + 1], ss[:tt, k:k + 1], [B_ss], [B_ss], power=0.5, bias=epsc[:tt, :], scale=1.0 / n)

                        P_src = [None]

                        def stage1(ti):
                            b, tl = ti // tpb, ti % tpb
                            if tl == 0:
                                modulate(s2, l, i, b, hTb, B_h, scr)
                            tsl = slice(tl * tt, (tl + 1) * tt)
                            k2 = ti % 2
                            cqn, latb, krb = cqn2[k2], latb2[k2], krb2[k2]
                            B_cqn, B_latb, B_krb = B_cqn2[k2], B_latb2[k2], B_krb2[k2]
                            b0, b1 = 2 * (ti % 2), 2 * (ti % 2) + 1
                            for kc in range(8):
                                mm(ps[b0][:tt, 0:384], hTb[:, kc, tsl], wB[:, kc, 0:384], kc == 0, kc == 7, [B_h, B_w], [P[b0]])
                            for kc in range(8):
                                mm(ps[b1][:tt, 0:320], hTb[:, kc, tsl], wB[:, kc, 384:704], kc == 0, kc == 7, [B_h, B_w], [P[b1]])
                            P_src[0] = P[b0]
                            rms_tok(ps[b0][:tt, 0:384], 384, gq_b, 0)
                            stt("dve", cqn[:tt, :], ps[b0][:tt, 0:384], ss[:tt, 0:1], gq_b[:tt, :], ALU.mult, ALU.mult,
                                [P[b0], B_ss, B_const], [B_cqn])
                            P_src[0] = P[b1]
                            rms_tok(ps[b1][:tt, 0:256], 256, gkv_b, 1)
                            stt("dve", stg_l[k2][:tt, :], ps[b1][:tt, 0:256], ss[:tt, 1:2], gkv_b[:tt, :], ALU.mult, ALU.mult,
                                [P[b1], B_ss, B_const], [B_sl[k2]])
                            tk.dma("sp", outs["lat"][ti * tt:(ti + 1) * tt, :], stg_l[k2][:tt, :], R=[B_sl[k2]])
                            tk.op("act", lambda e: e.copy(latb[:tt, :], stg_l[k2][:tt, :]), [B_sl[k2]], [B_latb])
                            tk.op("act", lambda e: e.copy(raw[:tt, 0:64], ps[b1][:tt, 256:320]), [P[b1]], [B_raw])
                            rope("pool", stg_r[k2][:tt, :], raw[:tt, 0:64], 1, ti, tmr, [B_raw], [B_sr[k2]], B_tmr)
                            tk.dma("sp", outs["kr"][ti * tt:(ti + 1) * tt, :], stg_r[k2][:tt, :], R=[B_sr[k2]])
                            tk.op("pool", lambda e: e.tensor_copy(krb[:tt, 0:64], stg_r[k2][:tt, :]), [B_sr[k2]], [B_krb])
                            tk.op("pool", lambda e: e.tensor_copy(krb[:tt, 64:128], stg_r[k2][:tt, :]), [B_sr[k2]], [B_krb])

                        def stage2(ti):
                            b, tl = ti // tpb, ti % tpb
                            tsl = slice(tl * tt, (tl + 1) * tt)
                            k2 = ti % 2
                            cqn, latb, krb = cqn2[k2], latb2[k2], krb2[k2]
                            B_cqn, B_latb, B_krb = B_cqn2[k2], B_latb2[k2], B_krb2[k2]
                            for c in range(3):
                                tr(psb[:, c * 128:c * 128 + tt], cqn[:tt, c * 128:(c + 1) * 128], identb[:tt, :tt],
                                   [B_cqn, B_const], [PB[0]])
                            for c in range(2):
                                tr(psb[:, 512 + c * 128:512 + c * 128 + tt], latb[:tt, c * 128:(c + 1) * 128], identb[:tt, :tt],
                                   [B_latb, B_const], [PB[1]])
                            tr(psb[:, 768:768 + tt], krb[:tt, :], identb[:tt, :tt], [B_krb, B_const], [PB[1]])
                            vcopy("dve", cqnT[:, :, tsl], psb[:, 0:384].rearrange("p (c t) -> p c t", c=3)[:, :, :tt],
                                  [PB[0]], [B_cqnT])
                            kc0 = Pa + ti * tt
                            tk.op("act", lambda e: e.copy(
                                latT[:, :, tl * tt:(tl + 1) * tt], psb[:, 512:768].rearrange("p (c t) -> p c t", c=2)[:, :, :tt]),
                                [PB[1]], [B_lat])
                            tk.op("act", lambda e: e.copy(krT[:, kc0:kc0 + tt], psb[:, 768:768 + tt]), [PB[1]], [B_kr])
                            if tl == tpb - 1:
                                for h in range(4):
                                    pb = 4 + h % 2
                                    for kc in range(3):
                                        mm(ps[pb][:, :bs], wuq_n[:, kc, h * 128:(h + 1) * 128], cqnT[:, kc, :bs],
                                           kc == 0, kc == 2, [B_w, B_cqnT], [P[pb]])
                                    vcopy("dve", bqnT[:, h, blk(b)], ps[pb][:, :bs], [P[pb]], [B_qn])
                                for tl2 in range(tpb):
                                    ti2 = b * tpb + tl2
                                    pb = 4 + tl2 % 2
                                    for kc in range(3):
                                        mm(ps[pb][:tt, 0:256], cqnT[:, kc, tl2 * tt:(tl2 + 1) * tt], wuq_r[:, kc, :],
                                           kc == 0, kc == 2, [B_w, B_cqnT], [P[pb]])
                                    tk.op("act", lambda e: e.copy(raw[:tt, :], ps[pb][:tt, 0:256]), [P[pb]], [B_raw])
                                    rope("pool", qrb[:tt, :], raw[:tt, :], 4, ti2, tmr, [B_raw], [B_qrb], B_tmr)
                                    for c in range(2):
                                        tr(psb[:, c * 128:c * 128 + tt], qrb[:tt, c * 128:(c + 1) * 128], identb[:tt, :tt],
                                           [B_qrb, B_const], [PB[0]])
                                    vcopy("dve", bqrT[:, :, ti2 * tt:(ti2 + 1) * tt],
                                          psb[:, 0:256].rearrange("p (c t) -> p c t", c=2)[:, :, :tt], [PB[0]], [B_qr])
                                kv_proj(Pa + b * bs, bs,
                                        [(Pa // 128 + b * tpb + j, j * tt, tt) for j in range(tpb)], 0)

                        stage1(0)
                        for ti in range(ntile):
                            if ti + 1 < ntile:
                                stage1(ti + 1)
                            stage2(ti)
                        tk.barrier()
                    if "B_attn" in KSKIP:
                        return
                    with ExitStack() as s2:
                        wout = sb(s2, "b_wout", [128, 8, D], BF16)
                        tk.dma("pool", wout[:], I["ab_w_out"].rearrange("(kc p) n -> p kc n", p=128), W=[B_wo])
                        pT = [sb(s2, "b_pT%d" % k, [128, 512], BF16) for k in range(4)]
                        BpT = [Buf() for _ in range(4)]
                        rr = sb(s2, "b_rr", [128, 512])
                        B_rr = Buf()
                        oTB = sb(s2, "b_oTB", [128, 4, 512], BF16)
                        B_oTB = Buf()
                        cntr = [0]
                        for b in range(nblk):
                            for h in range(4):
                                ob, db = 2 + h % 2, 4 + h % 2
                                rows = slice(64 * (h % 2), 64 * (h % 2) + 64)
                                attn_qblock(b, [(bknT[:, h, :], bqnT[:, h, :]), (krT[rows, :], bqrT[rows, h // 2, :])],
                                            lambda k, ks: bv[:ks, k, h * 128:(h + 1) * 128],
                                            kts, MLA_SCALE, ob, db, [B_qn, B_qr, B_kn, B_kr, B_bv], pT, BpT, cntr)
                                rpow(rr[:, :bs], ps[db][:, :bs], [P[db]], [B_rr])
                                tt_("dve", oTB[:, h, :bs], ps[ob][:, :bs], rr[:, :bs], ALU.mult, [P[ob], B_rr], [B_oTB])
                            for fo in range(8):
                                wb_ = 2 + fo % 4
                                for c in range(8):
                                    rhs = oTA[:, c, blk(b)] if c < 4 else oTB[:, c - 4, :bs]
                                    mm(ps[wb_][:, :bs], wout[:, c, fo * 128:(fo + 1) * 128], rhs, c == 0, c == 7,
                                       [B_wo, B_oTA, B_oTB], [P[wb_]])
                                stt("dve", xT[:, fo, blk(b)], ps[wb_][:, :bs], g_col(l, i, fo, s), xT[:, fo, blk(b)],
                                    ALU.mult, ALU.add, [P[wb_], B_mod], [B_x[b]])
                        tk.barrier()

            def phase_C():
                l, i = 1, 1
                kts = key_tiles(Pc)
                nkt = len(kts)
                npast = Pc // 128
                with ExitStack() as sc_:
                    hT = sb(sc_, "c_hT", [128, 8, T], BF16)
                    Bh = [Buf() for _ in range(nblk)]
                    biasT = sb(sc_, "biasT", [128, 16, 2, 128], BF16)
                    B_bias = Buf()
                    with ExitStack() as s0:
                        scr = mod_scratch(s0)
                        for b in range(nblk):
                            modulate(s0, l, i, b, hT[:, :, blk(b)], Bh[b], scr)
                        hk = sb(s0, "hk", [128, 16, 128])
                        B_rb0, B_hk = B_const, Buf()
                        for dl in range(2 if "C_bias" not in KSKIP else 0):
                            off = 128 if dl == 0 else 0
                            src = bass.AP(tensor=rbp.tensor, offset=off, ap=[[1, 128], [384, 16], [1, 128]])
                            tk.dma("sp", hk[:], src, W=[B_hk])
                            for h in range(16):
                                mm(ps[h % 4][:, 0:128], hk[:, h, :], antiid[:], True, True, [B_hk, B_const], [P[h % 4]])
                                ts_("dve", biasT[:, h, dl, :], ps[h % 4][:, 0:128], rb0[:, h:h + 1], 1.0 / C_SCALE,
                                    ALU.subtract, ALU.mult, [P[h % 4], B_rb0], [B_bias])
                        tk.barrier()
                    for hh in range(2):
                        with ExitStack() as s1:
                            qT = sb(s1, "c_qT", [128, 4, T], BF16)
                            kT = sb(s1, "c_kT", [128, 4, Pc + T], BF16)
                            v = sb(s1, "c_v", [128, nkt, 512], BF16)
                            B_q, B_k, B_v = Buf(), Buf(), Buf()
                            with ExitStack() as s2:
                                wq = sb(s2, "c_wq", [128, 8, 512], BF16)
                                wk = sb(s2, "c_wk", [128, 8, 512], BF16)
                                wv = sb(s2, "c_wv", [128, 8, 512], BF16)
                                B_w = Buf()
                                for wt, c0 in ((wq, 0), (wk, D), (wv, 2 * D)):
                                    tk.dma("pool", wt[:], I["c_w_in"][:, c0 + hh * 512:c0 + (hh + 1) * 512].rearrange(
                                        "(kc p) n -> p kc n", p=128), W=[B_w])
                                stg = [sb(s2, "c_st%d" % k, [128, 512]) for k in range(2)]
                                B_st = [Buf() for _ in range(2)]
                                if not is_p:
                                    ckb = sb(s2, "c_ckb", [128, 4, 512], BF16)
                                    B_ck = Buf()
                                    tk.dma("pool", ckb[:], caches["c_k"][:, hh * 512:(hh + 1) * 512].rearrange(
                                        "(t p) f -> p t f", p=128), W=[B_ck])
                                    tk.dma("pool", v[:, 0:4, :], caches["c_v"][:, hh * 512:(hh + 1) * 512].rearrange(
                                        "(t p) f -> p t f", p=128), W=[B_v])
                                    for t_ in range(4):
                                        pbi = t_ % 2
                                        for c in range(4):
                                            tr(psb[:, pbi * 512 + c * 128:pbi * 512 + (c + 1) * 128],
                                               ckb[:, t_, c * 128:(c + 1) * 128], identb[:], [B_ck, B_const], [PB[pbi]])
                                        vcopy("dve", kT[:, :, t_ * 128:(t_ + 1) * 128],
                                              psb[:, pbi * 512:(pbi + 1) * 512].rearrange("p (c t) -> p c t", c=4),
                                              [PB[pbi]], [B_k])
                                    if hh == 0 and "C_d2d" not in KSKIP:
                                        tk.dma("sp", outs["ck"][0:CPAST - DEC, :], caches["c_k"][DEC:CPAST, :])
                                        tk.dma("sp", outs["cv"][0:CPAST - DEC, :], caches["c_v"][DEC:CPAST, :])
                                sti = 0
                                for b in range(nblk):
                                    for c in range(4):
                                        for (wt, dst, Bd, off, pb) in ((wq, qT, B_q, 0, 0), (wk, kT, B_k, Pc, 1)):
                                            pbk = pb * 2 + c % 2
                                            for kc in range(8):
                                                mm(ps[pbk][:, :bs], wt[:, kc, c * 128:(c + 1) * 128], hT[:, kc, blk(b)],
                                                   kc == 0, kc == 7, [B_w, Bh[b]], [P[pbk]])
                                            dsl = dst[:, c, off + b * bs:off + (b + 1) * bs]
                                            if pb == 0:
                                                vcopy("dve", dsl, ps[pbk][:, :bs], [P[pbk]], [Bd])
                                            else:
                                                tk.op("act", lambda e: e.copy(dsl, ps[pbk][:, :bs]), [P[pbk]], [Bd])
                                    for tl in range(tpb):
                                        ti = b * tpb + tl
                                        tsl = slice(ti * tt, (ti + 1) * tt)
                                        pbk = 4 + ti % 2
                                        for kc in range(8):
                                            mm(ps[pbk][:tt, :], hT[:, kc, tsl], wv[:, kc, :], kc == 0, kc == 7, [Bh[b], B_w], [P[pbk]])
                                        vcopy("dve", v[:tt, npast + ti, :], ps[pbk][:tt, :], [P[pbk]], [B_v])
                                        orow = (ti * tt - (T - CPAST)) if is_p else (CPAST - DEC)
                                        if orow >= 0:
                                            k4 = sti % 2
                                            sti += 1
                                            tk.op("act", lambda e: e.copy(stg[k4][:tt, :], ps[pbk][:tt, :]), [P[pbk]], [B_st[k4]])
                                            tk.dma("sp", outs["cv"][orow:orow + tt, hh * 512:(hh + 1) * 512], stg[k4][:tt, :],
                                                   R=[B_st[k4]])
                                            for kc in range(8):
                                                mm(ps[6][:tt, :], hT[:, kc, tsl], wk[:, kc, :], kc == 0, kc == 7, [Bh[b], B_w], [P[6]])
                                            k4 = sti % 2
                                            sti += 1
                                            tk.op("act", lambda e: e.copy(stg[k4][:tt, :], ps[6][:tt, :]), [P[6]], [B_st[k4]])
                                            tk.dma("sp", outs["ck"][orow:orow + tt, hh * 512:(hh + 1) * 512], stg[k4][:tt, :],
                                                   R=[B_st[k4]])
                                tk.barrier()
                            with ExitStack() as s2:
                                oTC = sb(s2, "c_oT", [128, 4, T], BF16)
                                B_oTC = Buf()
                                wout = sb(s2, "c_wout", [128, 4, D], BF16)
                                B_wo = Buf()
                                tk.dma("pool", wout[:], I["c_w_out"][hh * 512:(hh + 1) * 512, :].rearrange(
                                    "(kc p) n -> p kc n", p=128), W=[B_wo])
                                pT = [sb(s2, "c_pT%d" % k, [128, 4, 128], BF16) for k in range(2)]
                                BpT = [Buf(), Buf()]
                                rr = sb(s2, "c_rr", [128, 4, 128])
                                B_rr = Buf()
                                units = []
                                for ti in range(ntile):
                                    if is_p:
                                        vis = [(k, k - ti) for k in range(max(0, ti - 4), ti + 1)]
                                    else:
                                        vis = [(k, None) for k in range(nkt)]
                                    for g2 in range(2):
                                        for n_, (k, dl) in enumerate(vis):
                                            units.append((ti, g2, n_, len(vis), k, dl))

                                def stage_s(u):
                                    ti, g2, n_, nv, k, dl = units[u]
                                    koff, ks = kts[k]
                                    sbk = u % 2
                                    if is_p:
                                        bidx = 0 if dl == 0 else (1 if dl == -1 else None)
                                    else:
                                        bidx = 1 if k == npast - 1 else (0 if k == npast else None)
                                    for hl in range(4):
                                        hd = 4 * g2 + hl
                                        par, slot = hl % 2, hl // 2
                                        sb_ = 2 * sbk + par
                                        c, rows = hd // 2, slice(64 * par, 64 * par + 64)
                                        so = ps[sb_][:ks, slot * 128:slot * 128 + tt]
                                        mm(so, kT[rows, c, koff:koff + ks], qT[rows, c, ti * tt:(ti + 1) * tt],
                                           True, bidx is None, [B_q, B_k], [P[sb_]])
                                        if bidx is not None:
                                            mm(so, identb[:ks, :ks], biasT[:ks, hh * 8 + hd, bidx, :tt],
                                               False, True, [B_const, B_bias], [P[sb_]])
                                    for par in range(2):
                                        sb_ = 2 * sbk + par
                                        act(pT[sbk][:ks, 2 * par:2 * par + 2, :tt],
                                            ps[sb_][:ks, 0:256].rearrange("p (h t) -> p h t", h=2)[:, :, :tt],
                                            AF.Exp, [P[sb_]], [BpT[sbk]], scale=C_SCALE)
                                    if is_p and dl == 0:
                                        mset("pool", pT[sbk][64:128, :, 0:64], 0.0, [BpT[sbk]])
                                    if is_p and dl == -4:
                                        mset("pool", pT[sbk][0:64, :, 64:128], 0.0, [BpT[sbk]])

                                def stage_do(u):
                                    ti, g2, n_, nv, k, dl = units[u]
                                    koff, ks = kts[k]
                                    sbk = u % 2
                                    ob, db = 4 + g2, 6 + g2
                                    dv_ = ps[db][:, :].rearrange("p (h t) -> p h t", h=4)
                                    ov_ = ps[ob][:, :].rearrange("p (h t) -> p h t", h=4)
                                    if tt == 128:
                                        mm(ps[db][:, :], ones_1[:ks, :], pT[sbk][:ks, :, :].rearrange("p h t -> p (h t)"),
                                           n_ == 0, n_ == nv - 1, [BpT[sbk], B_const], [P[db]])
                                    else:
                                        for j in range(4):
                                            mm(dv_[:, j, :tt], ones_1[:ks, :], pT[sbk][:ks, j, :tt],
                                               n_ == 0 and j == 0, n_ == nv - 1, [BpT[sbk], B_const], [P[db]], sgc=True)
                                    for hl in range(4):
                                        hd = 4 * g2 + hl
                                        j = (hl % 2) * 2 + hl // 2
                                        c = hd // 2
                                        mm(ov_[:, j, :tt], v[:ks, k, c * 128:(c + 1) * 128], pT[sbk][:ks, j, :tt],
                                           n_ == 0 and hl == 0, n_ == nv - 1, [BpT[sbk], B_v], [P[ob]], sgc=True)
                                    if n_ == nv - 1:
                                        rpow(rr[:, :, :tt], dv_[:, :, :tt], [P[db]], [B_rr])
                                        for hl in range(4):
                                            hd = 4 * g2 + hl
                                            j = (hl % 2) * 2 + hl // 2
                                            c, rows = hd // 2, slice(64 * (hd % 2), 64 * (hd % 2) + 64)
                                            tt_("dve", oTC[rows, c, ti * tt:(ti + 1) * tt], ov_[rows, j, :tt],
                                                rr[rows, j, :tt], ALU.mult, [P[ob], B_rr], [B_oTC])

                                stage_s(0)
                                for u in range(len(units)):
                                    if u + 1 < len(units):
                                        stage_s(u + 1)
                                    stage_do(u)
                                for b in range(nblk if "C_noout" not in KSKIP else 0):
                                    for fo in range(8):
                                        pb = fo % 2
                                        for c in range(4):
                                            mm(ps[pb][:, :bs], wout[:, c, fo * 128:(fo + 1) * 128], oTC[:, c, blk(b)], c == 0, c == 3,
                                               [B_wo, B_oTC], [P[pb]])
                                        stt("dve", xT[:, fo, blk(b)], ps[pb][:, :bs], g_col(l, i, fo, s), xT[:, fo, blk(b)],
                                            ALU.mult, ALU.add, [P[pb], B_mod], [B_x[b]])
                                tk.barrier()

            def final():
                with ExitStack() as sc_:
                    sq = sb(sc_, "y_sq", [128, 2, 512], BF16)
                    rstd = sb(sc_, "y_rstd", [128, 512])
                    yT = sb(sc_, "y_yT", [128, 8, 512])
                    stg = [sb(sc_, "y_st%d" % k, [128, D]) for k in range(2)]
                    Bs = [Buf() for _ in range(3)]
                    B_y = Buf()
                    B_st = [Buf(), Buf()]
                    for b in range(nblk):
                        for c in range(8):
                            act(sq[:, c % 2, :bs], xT[:, c, blk(b)], AF.Square, [B_x[b]], [Bs[c % 2]])
                            mm(ps[6][:, :bs], ones_d[:], sq[:, c % 2, :bs], c == 0, c == 7, [Bs[c % 2], B_const], [P[6]])
                        rpow(rstd[:, :bs], ps[6][:, :bs], [P[6]], [Bs[2]], power=0.5, bias=epsc[:])
                        for c in range(8):
                            stt("dve", yT[:, c, :bs], xT[:, c, blk(b)], fgT[:, c:c + 1], rstd[:, :bs], ALU.mult, ALU.mult,
                                [B_x[b], B_const, Bs[2]], [B_y])
                        for tl in range(tpb):
                            ti = b * tpb + tl
                            k2 = ti % 2
                            for g in range(2):
                                pb = g * 2 + k2
                                for c4 in range(4):
                                    c = g * 4 + c4
                                    tr(ps[pb][:tt, c4 * 128:(c4 + 1) * 128], yT[:, c, tl * tt:(tl + 1) * tt], ident[:],
                                       [B_y, B_const], [P[pb]])
                                if g == 0:
                                    vcopy("dve", stg[k2][:tt, 0:512], ps[pb][:tt, :], [P[pb]], [B_st[k2]])
                                else:
                                    tk.op("act", lambda e: e.copy(stg[k2][:tt, 512:1024], ps[pb][:tt, :]), [P[pb]], [B_st[k2]])
                            tk.dma("sp", outs["y"][ti * tt:(ti + 1) * tt, :], stg[k2][:tt, :], R=[B_st[k2]])
                    tk.barrier()

            nph = dbg_nph
            if nph >= 1:
                ffn(0, 0)
            if nph >= 2:
                with ExitStack() as sA:
                    oTA = sb(sA, "oTA", [128, 4, T], BF16)
                    B_oTA = Buf()
                    phase_A(oTA, B_oTA)
                    if nph >= 3:
                        phase_B(oTA, B_oTA)
            if nph >= 4:
                ffn(0, 1)
            if nph >= 5:
                ffn(1, 0)
            if nph >= 6:
                phase_C()
            if nph >= 7:
                ffn(1, 1)
            final()

        for sp_ in range(2):
            if dbg_pass is not None and sp_ not in dbg_pass:
                continue
            seq_pass(sp_, SEQ, I["x_p"][sp_], lambda ti: cs_pt[:, ti, :], None,
                     dict(y=O["y_p"][sp_], ak=O["ak_p"][sp_], av=O["av_p"][sp_], lat=O["lat_p"][sp_],
                          kr=O["kr_p"][sp_], ck=O["ck_p"][sp_], cv=O["cv_p"][sp_]))
        if dbg_pass is None or 2 in dbg_pass:
            seq_pass(2, DEC, I["x_s"], lambda ti: cs_st[:, 0, :],
                     dict(a_k=I["ca_k"], a_v=I["ca_v"], lat=I["c_lat"], kr=I["c_kr"], c_k=I["cc_k"], c_v=I["cc_v"]),
                     dict(y=O["y_s"], ak=O["ak_s"], av=O["av_s"], lat=O["lat_s"], kr=O["kr_s"], ck=O["ck_s"], cv=O["cv_s"]))
        tk.barrier()
    return nc


_NC = None
_DBG = ()
_NCORES = 8


def _rope_table(pos):
    half = 32
    inv = (1.0 / (10000.0 ** (np.arange(half, dtype=np.float32) * np.float32(2.0 / 64)))).astype(np.float32)
    ang = pos.astype(np.float32)[:, None] * inv[None, :]
    return np.concatenate([np.cos(ang), np.sin(ang)], axis=1).astype(np.float32)


def kernel(**inp):
    global _NC
    if _NC is None:
        _NC = _build(*_DBG)
    f = lambda a: np.ascontiguousarray(np.asarray(a, dtype=np.float32))
    shared = dict(
        ada_w=f(inp["ada_w"]), ada_b=f(inp["ada_b"]), norm_g=f(inp["norm_g"]),
        ffn_w_in=f(inp["ffn_w_in"]), ffn_w_out=f(inp["ffn_w_out"]),
        ab_w_in=f(inp["ab_w_in"][0]), a_lambda=f(inp["a_lambda"][0]).reshape(256), a_subln_g=f(inp["a_subln_g"][0]),
        mla_q_norm_g=f(inp["mla_q_norm_g"][0]), mla_w_uq=f(inp["mla_w_uq"][0]),
        mla_kv_norm_g=f(inp["mla_kv_norm_g"][0]), mla_w_ukv=f(inp["mla_w_ukv"][0]),
        ab_w_out=f(inp["ab_w_out"][0]), c_w_in=f(inp["c_w_in"][0]), c_rel_bias=f(inp["c_rel_bias"][0]),
        c_w_out=f(inp["c_w_out"][0]), final_norm_g=f(inp["final_norm_g"]),
        ident=np.eye(128, dtype=np.float32), antiid=np.ascontiguousarray(np.eye(128, dtype=np.float32)[::-1]),
        cs_p=_rope_table(np.arange(SEQ)), cs_s=_rope_table(PAST + np.arange(DEC)),
    )
    xp, xs = f(inp["x_prompt"]), f(inp["x_sample"])
    cp, cs = f(inp["c_prompt"]), f(inp["c_sample"])
    in_maps = []
    for k in range(8):
        m = dict(shared)
        m["x_p"] = xp[2 * k:2 * k + 2]
        m["x_s"] = xs[k]
        m["c_all"] = np.ascontiguousarray(np.concatenate([cp[2 * k:2 * k + 2], cs[k:k + 1]], axis=0))
        m["ca_k"] = f(inp["cache_a_k"][0, k]).reshape(PAST, 512)
        m["ca_v"] = f(inp["cache_a_v"][0, k]).reshape(PAST, 512)
        m["c_lat"] = f(inp["cache_mla_latent"][0, k])
        m["c_kr"] = f(inp["cache_mla_krope"][0, k])
        m["cc_k"] = f(inp["cache_c_k"][0, k]).reshape(CPAST, D)
        m["cc_v"] = f(inp["cache_c_v"][0, k]).reshape(CPAST, D)
        in_maps.append(m)
    ncore = _NCORES
    res = run_bass_kernel_spmd(_NC, in_maps[:ncore], core_ids=list(range(ncore)))
    R = list(res.results) + [res.results[0]] * (8 - ncore)

    def cat(name, shape):
        return np.concatenate([np.asarray(R[k][name], dtype=np.float32).reshape(shape) for k in range(8)], axis=0)

    y_p = cat("y_p", (2, SEQ, D))
    y_s = cat("y_s", (1, DEC, D))
    ak_p = cat("ak_p", (2, SEQ, 4, 2, 64))[None]
    ak_s = cat("ak_s", (1, DEC, 4, 2, 64))[None]
    av_p = cat("av_p", (2, SEQ, 4, 128))[None]
    av_s = cat("av_s", (1, DEC, 4, 128))[None]
    lat_p = cat("lat_p", (2, SEQ, 256))[None]
    lat_s = cat("lat_s", (1, DEC, 256))[None]
    kr_p = cat("kr_p", (2, SEQ, 64))[None]
    kr_s = cat("kr_s", (1, DEC, 64))[None]
    ck_p = cat("ck_p", (2, CPAST, 16, 64))[None]
    ck_s = cat("ck_s", (1, CPAST, 16, 64))[None]
    cv_p = cat("cv_p", (2, CPAST, 16, 64))[None]
    cv_s = cat("cv_s", (1, CPAST, 16, 64))[None]
    return (y_p, y_s, ak_p, ak_s, av_p, av_s, lat_p, lat_s, kr_p, kr_s, ck_p, ck_s, cv_p, cv_s)
```
